# Optimizing a Trainium2 kernel written in Bass

```python
import jax, jax.numpy as jnp
from jax import lax
import numpy as np

D_MODEL = 1024
BATCH = 8
SEQ = 8192
DEPTH = 2
DEC_BATCH = 8
DEC_SEQ = 16
PAST_LEN = 2048

CHUNK = 64
QBLOCK = 128
N_HEADS = 8
QK_NOPE_DIM = 128
QK_ROPE_DIM = 64
V_HEAD_DIM = 128
Q_LORA_RANK = 384
KV_LORA_RANK = 256
ROPE_THETA = 10000.0
ATTN_SCALE = (QK_NOPE_DIM + QK_ROPE_DIM) ** -0.5
CONV_CH = 1024
CONV_WIDTH = 31
D_FF = 2816
FFN_CONV_WIDTH = 3
EPS = 1e-6
IN_COLS = Q_LORA_RANK + KV_LORA_RANK + QK_ROPE_DIM + 2 * CONV_CH + 2 * D_MODEL
IN_SPLITS = [Q_LORA_RANK, Q_LORA_RANK + KV_LORA_RANK, Q_LORA_RANK + KV_LORA_RANK + QK_ROPE_DIM,
             Q_LORA_RANK + KV_LORA_RANK + QK_ROPE_DIM + 2 * CONV_CH]

kernel_name = 'streaming_mla_conformer_hybrid_step'


def rms_norm(x, g):
    xf = x.astype(jnp.float32)
    y = xf * lax.rsqrt(jnp.mean(xf * xf, axis=-1, keepdims=True) + EPS)
    return (y * g.astype(jnp.float32)).astype(x.dtype)


def layer_norm(x, g, b):
    xf = x.astype(jnp.float32)
    mu = jnp.mean(xf, axis=-1, keepdims=True)
    d = xf - mu
    y = d * lax.rsqrt(jnp.mean(d * d, axis=-1, keepdims=True) + EPS)
    return (y * g.astype(jnp.float32) + b.astype(jnp.float32)).astype(x.dtype)


def rope_tables(pos):
    half = QK_ROPE_DIM // 2
    inv = jnp.power(ROPE_THETA, -jnp.arange(half, dtype=jnp.float32) / half)
    ang = pos.astype(jnp.float32)[:, None] * inv[None, :]
    return jnp.cos(ang), jnp.sin(ang)


def apply_rope(x, cos, sin):
    xf = x.astype(jnp.float32)
    x1, x2 = jnp.split(xf, 2, axis=-1)
    out = jnp.concatenate([x1 * cos - x2 * sin, x2 * cos + x1 * sin], axis=-1)
    return out.astype(x.dtype)


def causal_dwconv(x, prev, w, b):
    xp = jnp.concatenate([prev.astype(x.dtype), x], axis=1)
    y = lax.conv_general_dilated(xp, w[:, None, :].astype(x.dtype), window_strides=(1,), padding='VALID',
                                 dimension_numbers=('NWC', 'WIO', 'NWC'), feature_group_count=x.shape[-1])
    return y + b.astype(x.dtype), xp[:, xp.shape[1] - (w.shape[0] - 1):]


def mla_attend(q_lat, q_pe, c_kv, k_pe, mask):
    s = jnp.einsum('bqhc,bsc->bhqs', q_lat, c_kv) + jnp.einsum('bqhr,bsr->bhqs', q_pe, k_pe)
    s = s.astype(jnp.float32) * ATTN_SCALE
    if mask is not None:
        s = jnp.where(mask[None, None], s, -jnp.inf)
    p = jax.nn.softmax(s, axis=-1).astype(c_kv.dtype)
    return jnp.einsum('bhqs,bsc->bqhc', p, c_kv)


def prompt_attention(q_lat, q_pe, c_kv, k_pe):
    B, T = q_lat.shape[0], q_lat.shape[1]
    nb = T // QBLOCK
    ql = q_lat.reshape(B, nb, QBLOCK, N_HEADS, KV_LORA_RANK).transpose(1, 0, 2, 3, 4)
    qp = q_pe.reshape(B, nb, QBLOCK, N_HEADS, QK_ROPE_DIM).transpose(1, 0, 2, 3, 4)
    key_chunk = jnp.arange(T) // CHUNK

    def block(args):
        qlb, qpb, i = args
        q_chunk = (i * QBLOCK + jnp.arange(QBLOCK)) // CHUNK
        mask = key_chunk[None, :] <= q_chunk[:, None]
        return mla_attend(qlb, qpb, c_kv, k_pe, mask)

    out = lax.map(block, (ql, qp, jnp.arange(nb)))
    return out.transpose(1, 0, 2, 3, 4).reshape(B, T, N_HEADS, KV_LORA_RANK)


def hybrid_layer(x, pos, cache_lat, cache_kpe, conv_prev, ffn_prev,
                 norm_mix, w_in, q_norm, w_qb, kv_norm, w_kvb, w_o_attn,
                 conv_dw, conv_dw_b, conv_ln_g, conv_ln_b, w_conv_out, w_out,
                 norm_ffn, w_up, ffn_dw, ffn_dw_b, w_down):
    B, T, _ = x.shape
    h = rms_norm(x, norm_mix)
    q_a, kv_a, k_pe, conv_in, gates = jnp.split(h @ w_in, IN_SPLITS, axis=-1)

    cos, sin = rope_tables(pos)
    q = (rms_norm(q_a, q_norm) @ w_qb).reshape(B, T, N_HEADS, QK_NOPE_DIM + QK_ROPE_DIM)
    q_nope, q_pe = q[..., :QK_NOPE_DIM], q[..., QK_NOPE_DIM:]
    q_pe = apply_rope(q_pe, cos[None, :, None], sin[None, :, None])
    c_kv = rms_norm(kv_a, kv_norm)
    k_pe = apply_rope(k_pe, cos[None], sin[None])
    w_kvb_h = w_kvb.reshape(KV_LORA_RANK, N_HEADS, QK_NOPE_DIM + V_HEAD_DIM)
    w_uk, w_uv = w_kvb_h[..., :QK_NOPE_DIM], w_kvb_h[..., QK_NOPE_DIM:]
    q_lat = jnp.einsum('bthn,chn->bthc', q_nope, w_uk)
    if cache_lat is None:
        o_lat = prompt_attention(q_lat, q_pe, c_kv, k_pe)
    else:
        c_all = jnp.concatenate([cache_lat.astype(c_kv.dtype), c_kv], axis=1)
        kpe_all = jnp.concatenate([cache_kpe.astype(k_pe.dtype), k_pe], axis=1)
        o_lat = mla_attend(q_lat, q_pe, c_all, kpe_all, None)
    attn = jnp.einsum('bthc,chv->bthv', o_lat, w_uv).reshape(B, T, N_HEADS * V_HEAD_DIM) @ w_o_attn

    a, g = jnp.split(conv_in, 2, axis=-1)
    u = a * jax.nn.sigmoid(g)
    if conv_prev is None:
        conv_prev = jnp.zeros((B, CONV_WIDTH - 1, CONV_CH), x.dtype)
    u, conv_state = causal_dwconv(u, conv_prev, conv_dw, conv_dw_b)
    conv = jax.nn.silu(layer_norm(u, conv_ln_g, conv_ln_b)) @ w_conv_out

    g_attn, g_conv = jnp.split(jax.nn.sigmoid(gates), 2, axis=-1)
    x = x + (g_attn * attn + g_conv * conv) @ w_out

    up = rms_norm(x, norm_ffn) @ w_up
    if ffn_prev is None:
        ffn_prev = jnp.zeros((B, FFN_CONV_WIDTH - 1, 2 * D_FF), x.dtype)
    up, ffn_state = causal_dwconv(up, ffn_prev, ffn_dw, ffn_dw_b)
    ga, v = jnp.split(up, 2, axis=-1)
    x = x + (jax.nn.silu(ga) * v) @ w_down
    return x, c_kv, k_pe, conv_state, ffn_state


def setup_inputs(seed: int = 0) -> dict:
    key = jax.random.key(seed)
    ks = jax.random.split(key, 26)

    def nrm(k, shape, scale):
        return jax.random.normal(k, shape, jnp.float32) * scale

    def gain(k, shape):
        return 1.0 + 0.01 * jax.random.normal(k, shape, jnp.float32)

    L = DEPTH
    return {
        'x_prompt': nrm(ks[0], (BATCH, SEQ, D_MODEL), 1.0),
        'x_sample': nrm(ks[1], (DEC_BATCH, DEC_SEQ, D_MODEL), 1.0),
        'cache_kv_latent': nrm(ks[2], (L, DEC_BATCH, PAST_LEN, KV_LORA_RANK), 1.0),
        'cache_k_rope': nrm(ks[3], (L, DEC_BATCH, PAST_LEN, QK_ROPE_DIM), 1.0),
        'state_conv': nrm(ks[4], (L, DEC_BATCH, CONV_WIDTH - 1, CONV_CH), 0.5),
        'state_ffn_conv': nrm(ks[5], (L, DEC_BATCH, FFN_CONV_WIDTH - 1, 2 * D_FF), 1.0),
        'norm_mix': gain(ks[6], (L, D_MODEL)),
        'w_in': nrm(ks[7], (L, D_MODEL, IN_COLS), D_MODEL ** -0.5),
        'q_norm': gain(ks[8], (L, Q_LORA_RANK)),
        'w_qb': nrm(ks[9], (L, Q_LORA_RANK, N_HEADS * (QK_NOPE_DIM + QK_ROPE_DIM)), Q_LORA_RANK ** -0.5),
        'kv_norm': gain(ks[10], (L, KV_LORA_RANK)),
        'w_kvb': nrm(ks[11], (L, KV_LORA_RANK, N_HEADS * (QK_NOPE_DIM + V_HEAD_DIM)), KV_LORA_RANK ** -0.5),
        'w_o_attn': nrm(ks[12], (L, N_HEADS * V_HEAD_DIM, D_MODEL), (N_HEADS * V_HEAD_DIM) ** -0.5),
        'conv_dw': nrm(ks[13], (L, CONV_WIDTH, CONV_CH), CONV_WIDTH ** -0.5),
        'conv_dw_b': nrm(ks[14], (L, CONV_CH), 0.01),
        'conv_ln_g': gain(ks[15], (L, CONV_CH)),
        'conv_ln_b': nrm(ks[16], (L, CONV_CH), 0.01),
        'w_conv_out': nrm(ks[17], (L, CONV_CH, D_MODEL), CONV_CH ** -0.5),
        'w_out': nrm(ks[18], (L, D_MODEL, D_MODEL), D_MODEL ** -0.5),
        'norm_ffn': gain(ks[19], (L, D_MODEL)),
        'w_up': nrm(ks[20], (L, D_MODEL, 2 * D_FF), D_MODEL ** -0.5),
        'ffn_dw': nrm(ks[21], (L, FFN_CONV_WIDTH, 2 * D_FF), FFN_CONV_WIDTH ** -0.5),
        'ffn_dw_b': nrm(ks[22], (L, 2 * D_FF), 0.01),
        'w_down': nrm(ks[23], (L, D_FF, D_MODEL), D_FF ** -0.5),
        'norm_final': gain(ks[24], (D_MODEL,)),
    }


def reference(x_prompt, x_sample, cache_kv_latent, cache_k_rope, state_conv, state_ffn_conv,
              norm_mix, w_in, q_norm, w_qb, kv_norm, w_kvb, w_o_attn,
              conv_dw, conv_dw_b, conv_ln_g, conv_ln_b, w_conv_out, w_out,
              norm_ffn, w_up, ffn_dw, ffn_dw_b, w_down, norm_final):
    pos_p = jnp.arange(x_prompt.shape[1])
    pos_s = cache_kv_latent.shape[2] + jnp.arange(x_sample.shape[1])
    xp, xs = x_prompt, x_sample
    p_lat, p_kpe, p_conv, p_ffn = [], [], [], []
    s_lat, s_kpe, s_conv, s_ffn = [], [], [], []
    for l in range(DEPTH):
        lw = [norm_mix[l], w_in[l], q_norm[l], w_qb[l], kv_norm[l], w_kvb[l], w_o_attn[l],
              conv_dw[l], conv_dw_b[l], conv_ln_g[l], conv_ln_b[l], w_conv_out[l], w_out[l],
              norm_ffn[l], w_up[l], ffn_dw[l], ffn_dw_b[l], w_down[l]]
        xp, c1, k1, cs1, fs1 = hybrid_layer(xp, pos_p, None, None, None, None, *lw)
        xs, c2, k2, cs2, fs2 = hybrid_layer(xs, pos_s, cache_kv_latent[l], cache_k_rope[l],
                                            state_conv[l], state_ffn_conv[l], *lw)
        p_lat.append(c1); p_kpe.append(k1); p_conv.append(cs1); p_ffn.append(fs1)
        s_lat.append(c2); s_kpe.append(k2); s_conv.append(cs2); s_ffn.append(fs2)
    y_prompt = rms_norm(xp, norm_final)
    y_sample = rms_norm(xs, norm_final)
    return (y_prompt, y_sample,
            jnp.stack(p_lat), jnp.stack(p_kpe), jnp.stack(p_conv), jnp.stack(p_ffn),
            jnp.stack(s_lat), jnp.stack(s_kpe), jnp.stack(s_conv), jnp.stack(s_ffn))
```

```python
import numpy as np
import concourse.bass as bass
import concourse.mybir as mybir
from concourse.bass_utils import run_bass_kernel_spmd

F32 = mybir.dt.float32
BF16 = mybir.dt.bfloat16
AF = mybir.ActivationFunctionType
ALU = mybir.AluOpType


class Cfg:
    D = 1024
    SEQ = 8192
    DEPTH = 2
    DEC_SEQ = 16
    PAST = 2048
    NH = 8
    NOPE = 128
    ROPE = 64
    VD = 128
    QL = 384
    KVL = 256
    CONV_W = 31
    DFF = 2816
    FW = 3
    EPS = 1e-6
    THETA = 10000.0
    TP = 256
    NCORES = 8


IN_COLS = 384 + 256 + 64 + 2048 + 2048
C_QA, C_KVA, C_KPE, C_CA, C_CG, C_GA, C_GC = 0, 384, 640, 704, 1728, 2752, 3776

V_NM, V_QN, V_KN, V_CB, V_LG, V_LB, V_NF, V_FB, V_CW, V_FW, V_FIN = 0, 8, 11, 13, 21, 29, 37, 45, 89, 337, 469
NV = 477


class Res:
    __slots__ = ("name", "w", "r", "dsem")

    def __init__(self, name):
        self.name = name
        self.w = None
        self.r = []
        self.dsem = None


class Prog:
    def __init__(self, nc):
        self.nc = nc
        self.eng = {"pe": nc.tensor, "act": nc.scalar, "dve": nc.vector, "pool": nc.gpsimd, "sp": nc.sync}
        self.sem = {}
        self.cnt = {}
        self.known = {e: {} for e in self.eng}
        self._keep = []
        for e in self.eng:
            self._mksem(e)
        self.nwait = 0
        self.nops = 0

    def _mksem(self, name):
        s = self.nc.semaphore("s_" + name)
        self.sem[name] = s.__enter__()
        self._keep.append(s)
        self.cnt[name] = 0

    def _deps(self, reads, writes):
        evs = {}
        for r in reads:
            ev = r.w
            if ev is not None and evs.get(ev[0], 0) < ev[1]:
                evs[ev[0]] = ev[1]
        for w in writes:
            ev = w.w
            if ev is not None and evs.get(ev[0], 0) < ev[1]:
                evs[ev[0]] = ev[1]
            for ev in w.r:
                if evs.get(ev[0], 0) < ev[1]:
                    evs[ev[0]] = ev[1]
        return evs

    def _wait(self, e, evs):
        kn = self.known[e]
        for k, v in evs.items():
            if e == "pe" and k == "pe":
                continue
            if kn.get(k, 0) >= v:
                continue
            self.eng[e].wait_ge(self.sem[k], v)
            self.nwait += 1
            kn[k] = v

    def _commit(self, ev, reads, writes):
        for r in reads:
            k = ev[0]
            r.r = [x for x in r.r if x[0] != k]
            r.r.append(ev)
        for w in writes:
            w.w = ev
            w.r = []

    def op(self, e, fn, reads=(), writes=()):
        self._wait(e, self._deps(reads, writes))
        inst = fn()
        if isinstance(inst, (list, tuple)):
            inst = inst[-1]
        self.cnt[e] += 1
        inst.then_inc(self.sem[e], 1)
        self._commit((e, self.cnt[e]), reads, writes)
        self.nops += 1

    def dma(self, q, sres, out, in_, reads=(), writes=()):
        self.dmas(q, sres, [(out, in_)], reads, writes)

    def dmas(self, q, sres, pairs, reads=(), writes=()):
        if sres.dsem is None:
            sres.dsem = "d%d_%s" % (len(self.sem), sres.name)
            self._mksem(sres.dsem)
        self._wait(q, self._deps(reads, writes))
        for out, in_ in pairs:
            inst = self.eng[q].dma_start(out=out, in_=in_)
            self.cnt[sres.dsem] += 16
            inst.then_inc(self.sem[sres.dsem], 16)
            self.nops += 1
        self._commit((sres.dsem, self.cnt[sres.dsem]), reads, writes)

    def finish(self, e="sp"):
        evs = {k: v for k, v in self.cnt.items() if v > 0}
        self._wait(e, evs)


def build_program(cfg):
    nc = bass.Bass("TRN2", target_bir_lowering=False)
    D, SEQ, L, TS, PAST, NH = cfg.D, cfg.SEQ, cfg.DEPTH, cfg.DEC_SEQ, cfg.PAST, cfg.NH
    TP = cfg.TP
    KC = D // 128
    NFF = cfg.DFF // 128
    NT = SEQ // TP
    CW = cfg.CONV_W
    HW_ = CW - 1
    SCALE = float((cfg.NOPE + cfg.ROPE) ** -0.5)
    EPS = cfg.EPS
    KEYMAX = max(SEQ, PAST + 128)
    NKT = KEYMAX // 128

    def din(name, shape, dt=F32):
        return nc.dram_tensor(name, list(shape), dt, kind="ExternalInput").ap()

    def dout(name, shape, dt=F32):
        return nc.dram_tensor(name, list(shape), dt, kind="ExternalOutput").ap()

    xT_p = din("xT_p", [D, SEQ])
    xT_s = din("xT_s", [D, TS])
    cacheT = din("cacheT", [L, cfg.KVL, PAST])
    cache_tok = din("cache_tok", [L, PAST, cfg.KVL])
    kcacheT = din("kcacheT", [L, cfg.ROPE, PAST])
    sconvT = din("sconvT", [L, D, HW_])
    sffnT = din("sffnT", [L, 2 * cfg.DFF, 2])
    vecs_d = din("vecs", [128, L, NV])
    w_in = din("w_in", [L, D, IN_COLS])
    w_qb = din("w_qb", [L, cfg.QL, NH * 192])
    w_kvb = din("w_kvb", [L, cfg.KVL, NH * 256])
    w_ukT_d = din("w_ukT", [L, 128, NH * 256])
    w_oa = din("w_o_attn", [L, D, D])
    w_co = din("w_conv_out", [L, D, D])
    w_out = din("w_out", [L, D, D])
    w_up = din("w_up", [L, D, 2 * cfg.DFF])
    w_down = din("w_down", [L, cfg.DFF, D])
    cos_p = din("cos_p", [64, SEQ])
    sin_p = din("sin_p", [64, SEQ])
    cos_s = din("cos_s", [64, TS])
    sin_s = din("sin_s", [64, TS])
    ident_d = din("ident", [128, 128])

    yT_p = dout("yT_p", [D, SEQ])
    yT_s = dout("yT_s", [D, TS])
    o_ckv_p = dout("o_ckv_p", [L, cfg.KVL, SEQ])
    o_kpe_p = dout("o_kpe_p", [L, cfg.ROPE, SEQ])
    o_cst_p = dout("o_cst_p", [L, D, HW_])
    o_fst_p = dout("o_fst_p", [L, 2 * cfg.DFF, 2])
    o_ckv_s = dout("o_ckv_s", [L, cfg.KVL, TS])
    o_kpe_s = dout("o_kpe_s", [L, cfg.ROPE, TS])
    o_cst_s = dout("o_cst_s", [L, D, HW_])
    o_fst_s = dout("o_fst_s", [L, 2 * cfg.DFF, 2])
    xmid = nc.dram_tensor("xmid", [D, SEQ], F32, kind="Internal").ap()
    NPIECE = 60 * L
    wsc = nc.dram_tensor("wsc", [NPIECE, 128, 4096], BF16, kind="Internal").ap()

    p = Prog(nc)

    def sb(name, shape, dt):
        return nc.sbuf_tensor(name, list(shape), dt).__enter__()

    ckvT = sb("ckvT", [128, 2, KEYMAX], BF16)
    ckvK = sb("ckvK", [128, NKT, 260], BF16)
    kpeT = sb("kpeT", [128, KEYMAX], BF16)
    r_ckvT = [Res("ckvT%d" % i) for i in range(NKT)]
    r_ckvK = [Res("ckvK%d" % i) for i in range(NKT)]
    r_kpeT = [Res("kpeT%d" % i) for i in range(NKT)]
    r_cacheall = Res("cacheall")

    vecs = sb("vecs_sb", [128, L, NV], F32)
    r_vecs = Res("vecs")
    ident = sb("ident_sb", [128, 128], F32)
    r_ident = Res("ident")
    ident_b = sb("ident_b", [128, 128], BF16)
    r_identb = Res("identb")
    otok = sb("otok", [128, 2, 256], BF16)
    r_otok = Res("otok")
    rdn = sb("rdn", [128, 2], F32)
    r_rdn = [Res("rdn0"), Res("rdn1")]
    ones_f = sb("ones_f", [128, 128], F32)
    r_ones = Res("ones")
    wukh = [sb("wukh%d" % i, [128, 256], BF16) for i in range(2)]
    r_wukh = [Res("wukh%d" % i) for i in range(2)]
    wuvh = [sb("wuvh%d" % i, [128, 256], BF16) for i in range(2)]
    r_wuvh = [Res("wuvh%d" % i) for i in range(2)]

    xT2 = [sb("xT_%d" % i, [128, KC, TP], F32) for i in range(2)]
    r_xT2 = [[Res("xT%d_%d" % (j, i)) for i in range(KC)] for j in range(2)]
    r_xTall2 = [Res("xTall%d" % j) for j in range(2)]
    hT2 = [sb("hT_%d" % i, [128, KC, TP], BF16) for i in range(2)]
    r_hT2 = [[Res("hT%d_%d" % (j, i)) for i in range(KC)] for j in range(2)]
    qkv_s = sb("qkv_s", [128, 5, TP], F32)
    r_qkv = [Res("qkv%d" % i) for i in range(5)]
    big = sb("big", [128, 24, TP], BF16)
    r_big = [Res("big%d" % i) for i in range(24)]
    ubuf = sb("ubuf", [128, KC, HW_ + TP], F32)
    r_ubuf = [Res("ubuf%d" % i) for i in range(KC)]
    ybuf = sb("ybuf", [128, KC, TP], F32)
    r_ybuf = [Res("ybuf%d" % i) for i in range(KC)]
    sq = [sb("sq%d" % i, [128, TP], F32) for i in range(2)]
    r_sq = [Res("sq%d" % i) for i in range(2)]
    rstd = sb("rstd", [128, TP], F32)
    r_rstd = Res("rstd")
    qan = sb("qan", [128, 3, TP], BF16)
    r_qan = Res("qan")
    ckvf = sb("ckvf", [128, 2, TP], F32)
    r_ckvf = Res("ckvf")
    ra = sb("ra", [64, TP], F32)
    rb = sb("rb", [64, TP], F32)
    r_ra, r_rb = Res("ra"), Res("rb")
    kpo, r_kpo = ra, r_ra
    cosT = sb("cosT", [64, TP], F32)
    sinT = sb("sinT", [64, TP], F32)
    r_rope = Res("rope")
    qn = [sb("qn%d" % i, [128, TP], BF16) for i in range(2)]
    r_qn = [Res("qn%d" % i) for i in range(2)]
    qlat = [sb("qlat%d" % i, [128, 2, TP], BF16) for i in range(2)]
    r_qlat = [Res("qlat%d" % i) for i in range(2)]
    qf = sb("qf", [64, TP], F32)
    r_qf = Res("qf")
    kpf, r_kpf = qf, r_qf
    qpe = [sb("qpe%d" % i, [128, TP], BF16) for i in range(2)]
    r_qpe = [Res("qpe%d" % i) for i in range(2)]
    PT = [sb("PT%d" % i, [128, TP], BF16) for i in range(3)]
    r_PT = [Res("PT%d" % i) for i in range(3)]
    olat = [sb("olat%d" % i, [128, 2, TP], BF16) for i in range(2)]
    r_olat = [Res("olat%d" % i) for i in range(2)]
    sg = [sb("sg%d" % i, [128, TP], F32) for i in range(3)]
    r_sg = [Res("sg%d" % i) for i in range(3)]
    upb = [sb("upb%d" % i, [128, TP + 2], F32) for i in range(6)]
    r_upb = [Res("upb%d" % i) for i in range(6)]
    st = [upb[i][:, 0:TP] for i in range(4)]
    r_st = r_upb[0:4]
    fh = sb("fh", [128, 2 * NFF, 2], F32)
    r_fh = [Res("fh%d" % i) for i in range(2 * NFF)]
    r_fhall = Res("fhall")
    WSLOT = 4096
    NSLOT = 4
    ws = [sb("ws%d" % i, [128, WSLOT], BF16) for i in range(NSLOT)]
    r_ws = [Res("ws%d" % i) for i in range(NSLOT)]

    ps = [nc.psum_tensor("ps%d" % i, [128, 512], F32).__enter__() for i in range(8)]
    r_ps = [Res("ps%d" % i) for i in range(8)]

    state = {"g": 0, "gset": list(range(8)), "ws": 0, "ev": 0}

    def galloc():
        gs = state["gset"]
        b = gs[state["g"] % len(gs)]
        state["g"] += 1
        return b

    V = nc.vector
    G = nc.gpsimd
    A = nc.scalar
    PE = nc.tensor

    def mm(out, lhsT, rhs, start, stop):
        return PE.matmul(out, lhsT=lhsT, rhs=rhs, start=start, stop=stop, skip_group_check=True)

    pieces = {}

    def cached_load(key, dst_flat, dst_res, pairs, total):
        if key not in pieces:
            idx = len(pieces)
            assert idx < NPIECE
            pr = Res("piece%d" % idx)
            pieces[key] = (idx, pr)
            p.dmas("pool", dst_res, pairs, writes=[dst_res])
            p.dma("sp", dst_res, wsc[idx, :, 0:total], dst_flat[:, 0:total], reads=[dst_res], writes=[pr])
        else:
            idx, pr = pieces[key]
            p.dma("sp", dst_res, dst_flat[:, 0:total], wsc[idx, :, 0:total], reads=[pr], writes=[dst_res])

    def wload(key, segs):
        i = state["ws"] % NSLOT
        state["ws"] += 1
        off = 0
        views = []
        pairs = []
        for src in segs:
            shp = src.shape
            K, n = shp[1], shp[2]
            v = ws[i][:, off:off + K * n].rearrange("p (k n) -> p k n", k=K)
            pairs.append((v, src))
            off += K * n
            views.append(v)
        assert off <= WSLOT
        cached_load(key, ws[i], r_ws[i], pairs, off)
        return views, r_ws[i]

    def wcols(w_l, c0, n):
        return w_l.rearrange("(k p) n -> p k n", p=128)[:, :, c0:c0 + n]

    p.dma("sp", r_vecs, vecs[:], vecs_d, writes=[r_vecs])
    p.dma("sp", r_ident, ident[:], ident_d, writes=[r_ident])
    p.op("dve", lambda: V.memset(ones_f[:], 1.0), writes=[r_ones])
    p.op("dve", lambda: V.memset(kpeT[64:128, :], 0.0), writes=r_kpeT + [r_cacheall])
    p.op("dve", lambda: V.memset(ckvK[:, :, 256:257], 1.0), writes=r_ckvK + [r_cacheall])
    p.op("dve", lambda: V.tensor_copy(out=ident_b[:], in_=ident[:]), reads=[r_ident], writes=[r_identb])
    for i in range(2):
        p.op("dve", lambda: V.memset(qpe[i][64:128, :], 0.0), writes=[r_qpe[i]])

    def vcol(l, c):
        return vecs[:, l, c:c + 1]

    def rms_rstd(l, srcs, src_res, Dn, T):
        b = galloc()
        n = len(srcs)
        for i, a in enumerate(srcs):
            s = i % 2
            p.op("act", lambda: A.activation(out=sq[s][:, :T], in_=a, func=AF.Square),
                 reads=[src_res[i]], writes=[r_sq[s]])
            p.op("pe", lambda: mm(ps[b][:, :T], ones_f[:], sq[s][:, :T], i == 0, i == n - 1),
                 reads=[r_sq[s], r_ones], writes=[r_ps[b]])
        p.op("act", lambda: A.activation(out=rstd[:, :T], in_=ps[b][:, :T], func=AF.Ln, scale=1.0 / Dn, bias=EPS),
             reads=[r_ps[b]], writes=[r_rstd])
        p.op("act", lambda: A.activation(out=rstd[:, :T], in_=rstd[:, :T], func=AF.Exp, scale=-0.5),
             reads=[r_rstd], writes=[r_rstd])

    def rope(src, r_src, dst, r_dst, T):
        p.op("dve", lambda: V.tensor_tensor(out=ra[:, :T], in0=src[:, :T], in1=cosT[:, :T], op=ALU.mult),
             reads=[r_src, r_rope], writes=[r_ra])
        p.op("dve", lambda: [V.tensor_tensor(out=rb[0:32, :T], in0=src[32:64, :T], in1=sinT[32:64, :T], op=ALU.mult),
                             V.tensor_tensor(out=rb[32:64, :T], in0=src[0:32, :T], in1=sinT[0:32, :T], op=ALU.mult)],
             reads=[r_src, r_rope], writes=[r_rb])
        p.op("dve", lambda: V.tensor_tensor(out=dst[0:64, :T], in0=ra[:, :T], in1=rb[:, :T], op=ALU.add),
             reads=[r_ra, r_rb], writes=[r_dst])

    def proj(bank, cols, M, wv, c0, xs, x_res, w_res, T, extra_reads=()):
        K = len(xs)
        p.op("pe", lambda: [mm(ps[bank][0:M, cols], wv[:, k, c0:c0 + M], xs[k], k == 0, k == K - 1) for k in range(K)],
             reads=[w_res] + list(x_res) + list(extra_reads), writes=[r_ps[bank]])

    def tile_pass(l, t, mode):
        prompt = mode == "p"
        T = TP if prompt else TS
        tok0 = t * T if prompt else 0
        kbase = tok0 if prompt else PAST
        last_tile = (t == NT - 1) if prompt else True
        bi = (t % 2) if prompt else 0
        xT, r_xT, r_xTall, hT, r_hT = xT2[bi], r_xT2[bi], r_xTall2[bi], hT2[bi], r_hT2[bi]
        xs_f = [xT[:, k, :T] for k in range(KC)]
        hs = [hT[:, k, :T] for k in range(KC)]
        W_in = w_in[l]

        if l == 0:
            src = (xT_p if prompt else xT_s).rearrange("(k p) s -> p k s", p=128)[:, :, tok0:tok0 + T]
            p.dma("pool", r_xTall, xT[:, :, :T], src, writes=r_xT + [r_xTall])
        elif prompt:
            src = xmid.rearrange("(k p) s -> p k s", p=128)[:, :, tok0:tok0 + T]
            p.dma("pool", r_xTall, xT[:, :, :T], src, reads=[r_xmid[t]], writes=r_xT + [r_xTall])
        else:
            p.op("act", lambda: A.copy(out=xT[:, :, :T], in_=xs_keep[:, :, :]), reads=[r_xskeep], writes=r_xT + [r_xTall])
        cs, sn = (cos_p, sin_p) if prompt else (cos_s, sin_s)
        p.dmas("pool", r_rope, [(cosT[:, :T], cs[:, tok0:tok0 + T]), (sinT[:, :T], sn[:, tok0:tok0 + T])], writes=[r_rope])

        rms_rstd(l, xs_f, r_xT, D, T)
        for k in range(KC):
            p.op("dve", lambda: V.scalar_tensor_tensor(out=hs[k], in0=xs_f[k], scalar=vcol(l, V_NM + k), in1=rstd[:, :T],
                                                       op0=ALU.mult, op1=ALU.mult),
                 reads=[r_xT[k], r_rstd, r_vecs], writes=[r_hT[k]])
        yield None

        (wv,), wr = wload((l, "qa"), [wcols(W_in, C_QA, 384)])
        qa = [qkv_s[:, j, :T] for j in range(3)]
        for j in range(3):
            b = galloc()
            proj(b, slice(0, T), 128, wv, j * 128, hs, r_hT, wr, T)
            p.op("act", lambda: A.copy(out=qa[j], in_=ps[b][:, :T]), reads=[r_ps[b]], writes=[r_qkv[j]])
        rms_rstd(l, qa, r_qkv[0:3], cfg.QL, T)
        for j in range(3):
            p.op("dve", lambda: V.scalar_tensor_tensor(out=qan[:, j, :T], in0=qa[j], scalar=vcol(l, V_QN + j), in1=rstd[:, :T],
                                                       op0=ALU.mult, op1=ALU.mult),
                 reads=[r_qkv[j], r_rstd, r_vecs], writes=[r_qan])
        yield None

        (wv,), wr = wload((l, "kva"), [wcols(W_in, C_KVA, 320)])
        kva = [qkv_s[:, 3 + j, :T] for j in range(2)]
        for j in range(2):
            b = galloc()
            proj(b, slice(0, T), 128, wv, j * 128, hs, r_hT, wr, T)
            p.op("act", lambda: A.copy(out=kva[j], in_=ps[b][:, :T]), reads=[r_ps[b]], writes=[r_qkv[3 + j]])
        b = galloc()
        proj(b, slice(0, T), 64, wv, 256, hs, r_hT, wr, T)
        p.op("act", lambda: A.copy(out=kpf[:, :T], in_=ps[b][0:64, :T]), reads=[r_ps[b]], writes=[r_kpf])
        yield None
        rms_rstd(l, kva, r_qkv[3:5], cfg.KVL, T)
        for j in range(2):
            p.op("dve", lambda: V.scalar_tensor_tensor(out=ckvf[:, j, :T], in0=kva[j], scalar=vcol(l, V_KN + j), in1=rstd[:, :T],
                                                       op0=ALU.mult, op1=ALU.mult),
                 reads=[r_qkv[3 + j], r_rstd, r_vecs], writes=[r_ckvf])
        yield None
        o_ckv = (o_ckv_p if prompt else o_ckv_s)[l]
        o_kpe = (o_kpe_p if prompt else o_kpe_s)[l]
        p.dma("pool", r_ckvf, o_ckv.rearrange("(k p) s -> p k s", p=128)[:, :, tok0:tok0 + T], ckvf[:, :, :T], reads=[r_ckvf])
        nkt_new = (T + 127) // 128
        kt0 = kbase // 128
        new_res = []
        for s in range(nkt_new):
            new_res += [r_ckvT[kt0 + s], r_ckvK[kt0 + s], r_kpeT[kt0 + s]]
        p.op("act", lambda: A.copy(out=ckvT[:, :, kbase:kbase + T], in_=ckvf[:, :, :T]), reads=[r_ckvf],
             writes=[r_ckvT[kt0 + s] for s in range(nkt_new)] + [r_cacheall])
        for s in range(nkt_new):
            n = min(128, T - s * 128)
            b = galloc()
            p.op("pe", lambda: [PE.transpose(out=ps[b][0:n, j * 128:(j + 1) * 128], in_=ckvf[:, j, s * 128:s * 128 + n], identity=ident[:])
                                for j in range(2)], reads=[r_ckvf, r_ident], writes=[r_ps[b]])
            p.op("act", lambda: A.copy(out=ckvK[0:n, kt0 + s, 0:256], in_=ps[b][0:n, 0:256]), reads=[r_ps[b]],
                 writes=[r_ckvK[kt0 + s], r_cacheall])
        yield None
        rope(kpf, r_kpf, kpo, r_kpo, T)
        p.dma("pool", r_kpo, o_kpe[:, tok0:tok0 + T], kpo[:, :T], reads=[r_kpo])
        p.op("act", lambda: A.copy(out=kpeT[0:64, kbase:kbase + T], in_=kpo[:, :T]), reads=[r_kpo],
             writes=[r_kpeT[kt0 + s] for s in range(nkt_new)] + [r_cacheall])

        yield None
        taps = []

        def drain(n):
            for _ in range(min(n, len(taps))):
                taps.pop(0)()

        for i in range(KC // 2):
            (wa, wg), wr = wload((l, "cv", i), [wcols(W_in, C_CA + 256 * i, 256), wcols(W_in, C_CG + 256 * i, 256)])
            for cc in range(2):
                c = 2 * i + cc
                bA = galloc()
                proj(bA, slice(0, T), 128, wa, cc * 128, hs, r_hT, wr, T)
                bG = galloc()
                proj(bG, slice(0, T), 128, wg, cc * 128, hs, r_hT, wr, T)
                p.op("act", lambda: A.activation(out=qkv_s[:, cc, :T], in_=ps[bG][:, :T], func=AF.Tanh, scale=0.5), reads=[r_ps[bG]], writes=[r_qkv[cc]])
                p.op("act", lambda: A.activation(out=qkv_s[:, cc, :T], in_=qkv_s[:, cc, :T], func=AF.Identity, scale=0.5, bias=0.5), reads=[r_qkv[cc]], writes=[r_qkv[cc]])
                if prompt and t > 0:
                    p.op("dve", lambda: V.tensor_copy(out=ubuf[:, c, 0:HW_], in_=ubuf[:, c, TP:TP + HW_]),
                         reads=[r_ubuf[c]], writes=[r_ubuf[c]])
                p.op("dve", lambda: V.tensor_tensor(out=ubuf[:, c, HW_:HW_ + T], in0=ps[bA][:, :T], in1=qkv_s[:, cc, :T], op=ALU.mult),
                     reads=[r_ps[bA], r_qkv[cc]], writes=[r_ubuf[c]])
            def mk_tap(k, c):
                wk = vcol(l, V_CW + k * 8 + c)
                if k == 0:
                    return lambda: p.op("dve", lambda: V.tensor_scalar(out=ybuf[:, c, :T], in0=ubuf[:, c, 0:T], scalar1=wk,
                                                                       scalar2=vcol(l, V_CB + c), op0=ALU.mult, op1=ALU.add),
                                        reads=[r_ubuf[c], r_vecs], writes=[r_ybuf[c]])
                return lambda: p.op("dve", lambda: V.scalar_tensor_tensor(out=ybuf[:, c, :T], in0=ubuf[:, c, k:k + T], scalar=wk,
                                                                          in1=ybuf[:, c, :T], op0=ALU.mult, op1=ALU.add),
                                    reads=[r_ubuf[c], r_ybuf[c]], writes=[r_ybuf[c]])
            for k in range(CW):
                for cc in range(2):
                    taps.append(mk_tap(k, 2 * i + cc))
            yield None
        yield "SEG2"
        state["gset"] = [7]
        S_B = [0, 1, 6]
        ACC = [(2, 3), (4, 5)]
        if prompt:
            nkt = (tok0 + T) // 128
            ktiles = [(kt, 128) for kt in range(nkt)]
            diag0 = tok0 // 128
        else:
            ktiles = [(kt, 128) for kt in range(PAST // 128)] + [(PAST // 128, TS)]
            diag0 = 10 ** 9
        wq_views = {}
        qan_s = [qan[:, j, :T] for j in range(3)]

        def prepA(h):
            hl = h % 4
            if hl == 0:
                (wvq,), wrq = wload((l, "qb", h), [wcols(w_qb[l], h * 192, 768)])
                wq_views["v"] = (wvq, wrq)
            wvq, wrq = wq_views["v"]
            i2 = h % 2
            cached_load((l, "uk", h), wukh[i2], r_wukh[i2], [(wukh[i2][:, :], w_ukT_d[l][:, h * 256:(h + 1) * 256])], 256)
            cached_load((l, "uv", h), wuvh[i2], r_wuvh[i2],
                        [(wuvh[i2][:, k * 128:(k + 1) * 128], w_kvb[l][k * 128:(k + 1) * 128, h * 256 + 128:h * 256 + 256])
                         for k in range(2)], 256)
            b = galloc()
            proj(b, slice(0, T), 128, wvq, hl * 192, qan_s, [r_qan], wrq, T)
            p.op("act", lambda: A.copy(out=qn[i2][:, :T], in_=ps[b][:, :T]), reads=[r_ps[b]], writes=[r_qn[i2]])
            b = galloc()
            proj(b, slice(0, T), 64, wvq, hl * 192 + 128, qan_s, [r_qan], wrq, T)
            p.op("act", lambda: A.copy(out=qf[:, :T], in_=ps[b][0:64, :T]), reads=[r_ps[b]], writes=[r_qf])

        def prepB(h):
            i2 = h % 2
            b = galloc()
            p.op("pe", lambda: [mm(ps[b][:, j * T:(j + 1) * T], wukh[i2][:, j * 128:(j + 1) * 128], qn[i2][:, :T], True, True)
                                for j in range(2)], reads=[r_qn[i2], r_wukh[i2]], writes=[r_ps[b]])
            p.op("act", lambda: A.copy(out=qlat[i2][:, :, :T], in_=ps[b][:, 0:2 * T].rearrange("p (j t) -> p j t", j=2)),
                 reads=[r_ps[b]], writes=[r_qlat[i2]])
            rope(qf, r_qf, qpe[i2], r_qpe[i2], T)

        def score(h, idx):
            kt, nk = ktiles[idx]
            i2 = h % 2
            sb_ = S_B[idx % 3]
            j = kt - diag0
            c0 = 128 * j if j > 0 else 0
            k0 = kt * 128
            p.op("pe", lambda: [mm(ps[sb_][0:nk, c0:T], ckvT[:, 0, k0:k0 + nk], qlat[i2][:, 0, c0:T], True, False),
                                mm(ps[sb_][0:nk, c0:T], ckvT[:, 1, k0:k0 + nk], qlat[i2][:, 1, c0:T], False, False),
                                mm(ps[sb_][0:nk, c0:T], kpeT[:, k0:k0 + nk], qpe[i2][:, c0:T], False, True)],
                 reads=[r_ckvT[kt], r_kpeT[kt], r_qlat[i2], r_qpe[i2]], writes=[r_ps[sb_]])
            pi = idx % 3
            p.op("act", lambda: A.activation(out=PT[pi][0:nk, c0:T], in_=ps[sb_][0:nk, c0:T], func=AF.Exp, scale=SCALE),
                 reads=[r_ps[sb_]], writes=[r_PT[pi]])
            if j >= 0:
                p.op("dve", lambda: V.memset(PT[pi][64:128, c0:c0 + 64], 0.0), writes=[r_PT[pi]])

        NQC = (T + 127) // 128

        def pv(h, idx):
            kt, nk = ktiles[idx]
            j = kt - diag0
            pi = idx % 3
            first = idx == 0
            banks = ACC[h % 2]
            mms = []
            for qc in range(NQC):
                if j > 0 and qc < j:
                    continue
                qn = min(128, T - qc * 128)
                mms.append((qc, qn))
            p.op("pe", lambda: [mm(ps[banks[qc]][0:qn, 0:257], PT[pi][0:nk, qc * 128:qc * 128 + qn], ckvK[0:nk, kt, 0:257], first, True)
                                for qc, qn in mms],
                 reads=[r_ckvK[kt], r_PT[pi]], writes=[r_ps[banks[qc]] for qc, _ in mms])

        def finish_head(h):
            i2 = h % 2
            banks = ACC[h % 2]
            for qc in range(NQC):
                qn = min(128, T - qc * 128)
                bq = banks[qc]
                p.op("dve", lambda: V.reciprocal(out=rdn[0:qn, qc:qc + 1], in_=ps[bq][0:qn, 256:257]), reads=[r_ps[bq]], writes=[r_rdn[qc]])
                p.op("act", lambda: A.activation(out=otok[0:qn, qc, :], in_=ps[bq][0:qn, 0:256], func=AF.Copy, scale=rdn[0:qn, qc:qc + 1]),
                     reads=[r_ps[bq], r_rdn[qc]], writes=[r_otok])

        def tr_stage(h):
            i2 = h % 2
            b = galloc()
            pv_ = ps[b][:, 0:256].bitcast(BF16)
            p.op("pe", lambda: [PE.transpose(out=pv_[:, cj * T + qc * 128:cj * T + qc * 128 + min(128, T - qc * 128)],
                                             in_=otok[0:min(128, T - qc * 128), qc, cj * 128:(cj + 1) * 128],
                                             identity=ident_b[0:min(128, T - qc * 128), 0:min(128, T - qc * 128)])
                                for cj in range(2) for qc in range(NQC)],
                 reads=[r_otok, r_identb], writes=[r_ps[b]])
            p.op("act", lambda: A.copy(out=olat[i2][:, :, :T], in_=pv_[:, 0:2 * T].rearrange("p (j t) -> p j t", j=2)),
                 reads=[r_ps[b]], writes=[r_olat[i2]])

        def wuv_stage(h):
            i2 = h % 2
            b = galloc()
            p.op("pe", lambda: [mm(ps[b][:, :T], wuvh[i2][:, 0:128], olat[i2][:, 0, :T], True, False),
                                mm(ps[b][:, :T], wuvh[i2][:, 128:256], olat[i2][:, 1, :T], False, True)],
                 reads=[r_wuvh[i2], r_olat[i2]], writes=[r_ps[b]])
            p.op("act", lambda: A.copy(out=big[:, h, :T], in_=ps[b][:, :T]), reads=[r_ps[b]], writes=[r_big[h]])

        prepA(0)
        prepB(0)
        nk_ = len(ktiles)
        quota = -(-len(taps) // (NH * nk_))
        i_t, i_c, i_a, i_b = min(1, nk_ - 1), min(3, nk_ - 1), min(4, nk_ - 1), min(6, nk_ - 1)
        for h in range(NH):
            score(h, 0)
            if nk_ > 1:
                score(h, 1)
            for idx in range(nk_):
                if idx + 2 < nk_:
                    score(h, idx + 2)
                if idx == i_t and h > 0:
                    tr_stage(h - 1)
                if idx == i_c and h > 0:
                    wuv_stage(h - 1)
                if idx == i_a and h + 1 < NH:
                    prepA(h + 1)
                if idx == i_b and h + 1 < NH:
                    prepB(h + 1)
                pv(h, idx)
                drain(quota)
            finish_head(h)
        tr_stage(NH - 1)
        wuv_stage(NH - 1)
        state["gset"] = list(range(8))

        drain(10 ** 6)
        yield "SEG3"
        if last_tile:
            o_cst = (o_cst_p if prompt else o_cst_s)[l]
            p.dma("pool", r_ubuf[0], o_cst.rearrange("(k p) s -> p k s", p=128), ubuf[:, :, T:T + HW_], reads=r_ubuf)
        b1 = galloc()
        b2 = galloc()
        for c in range(KC):
            s = c % 2
            p.op("pe", lambda: mm(ps[b1][:, :T], ones_f[:], ybuf[:, c, :T], c == 0, c == KC - 1),
                 reads=[r_ybuf[c], r_ones], writes=[r_ps[b1]])
            p.op("act", lambda: A.activation(out=sq[s][:, :T], in_=ybuf[:, c, :T], func=AF.Square),
                 reads=[r_ybuf[c]], writes=[r_sq[s]])
            p.op("pe", lambda: mm(ps[b2][:, :T], ones_f[:], sq[s][:, :T], c == 0, c == KC - 1),
                 reads=[r_sq[s], r_ones], writes=[r_ps[b2]])
        mean, msq, var, nmr = st[0], st[1], st[2], st[3]
        p.op("dve", lambda: V.tensor_scalar(out=mean[:, :T], in0=ps[b1][:, :T], scalar1=1.0 / D, scalar2=None, op0=ALU.mult),
             reads=[r_ps[b1]], writes=[r_st[0]])
        p.op("dve", lambda: V.tensor_tensor(out=msq[:, :T], in0=mean[:, :T], in1=mean[:, :T], op=ALU.mult),
             reads=[r_st[0]], writes=[r_st[1]])
        p.op("dve", lambda: V.scalar_tensor_tensor(out=var[:, :T], in0=ps[b2][:, :T], scalar=1.0 / D, in1=msq[:, :T],
                                                   op0=ALU.mult, op1=ALU.subtract), reads=[r_ps[b2], r_st[1]], writes=[r_st[2]])
        lrs, r_lrs = st[1], r_st[1]
        p.op("act", lambda: A.activation(out=lrs[:, :T], in_=var[:, :T], func=AF.Ln, bias=EPS), reads=[r_st[2]], writes=[r_lrs])
        p.op("act", lambda: A.activation(out=lrs[:, :T], in_=lrs[:, :T], func=AF.Exp, scale=-0.5), reads=[r_lrs], writes=[r_lrs])
        p.op("dve", lambda: V.scalar_tensor_tensor(out=nmr[:, :T], in0=mean[:, :T], scalar=-1.0, in1=lrs[:, :T],
                                                   op0=ALU.mult, op1=ALU.mult), reads=[r_st[0], r_lrs], writes=[r_st[3]])
        for c in range(KC):
            p.op("dve", lambda: V.tensor_tensor(out=ybuf[:, c, :T], in0=ybuf[:, c, :T], in1=lrs[:, :T], op=ALU.mult),
                 reads=[r_ybuf[c], r_lrs], writes=[r_ybuf[c]])
            p.op("dve", lambda: V.tensor_tensor(out=ybuf[:, c, :T], in0=ybuf[:, c, :T], in1=nmr[:, :T], op=ALU.add),
                 reads=[r_ybuf[c], r_st[3]], writes=[r_ybuf[c]])
            p.op("act", lambda: A.activation(out=big[:, 8 + c, :T], in_=ybuf[:, c, :T], func=AF.Silu,
                                             scale=vcol(l, V_LG + c), bias=vcol(l, V_LB + c)),
                 reads=[r_ybuf[c], r_vecs], writes=[r_big[8 + c]])
            yield None

        for n2 in range(KC // 2):
            (wo, wc), wr1 = wload((l, "oc", n2), [wcols(w_oa[l], n2 * 256, 256), wcols(w_co[l], n2 * 256, 256)])
            (wga, wgc), wr2 = wload((l, "gt", n2), [wcols(W_in, C_GA + n2 * 256, 256), wcols(W_in, C_GC + n2 * 256, 256)])
            for cc in range(2):
                n = 2 * n2 + cc
                bA, bC, bGa, bGc = galloc(), galloc(), galloc(), galloc()
                srcs = [(bA, wo, [big[:, h, :T] for h in range(8)]), (bC, wc, [big[:, 8 + c, :T] for c in range(8)]),
                        (bGa, wga, hs), (bGc, wgc, hs)]
                p.op("pe", lambda: [mm(ps[bk][:, 0:T], wsrc[:, k, cc * 128:cc * 128 + 128], xs_[k], k == 0, k == KC - 1)
                                    for bk, wsrc, xs_ in srcs for k in range(KC)],
                     reads=[wr1, wr2] + r_big[0:16] + list(r_hT), writes=[r_ps[bA], r_ps[bC], r_ps[bGa], r_ps[bGc]])
                p.op("act", lambda: A.activation(out=sg[0][:, :T], in_=ps[bGa][:, :T], func=AF.Tanh, scale=0.5), reads=[r_ps[bGa]], writes=[r_sg[0]])
                p.op("act", lambda: A.activation(out=sg[0][:, :T], in_=sg[0][:, :T], func=AF.Identity, scale=0.5, bias=0.5), reads=[r_sg[0]], writes=[r_sg[0]])
                p.op("act", lambda: A.activation(out=sg[1][:, :T], in_=ps[bGc][:, :T], func=AF.Tanh, scale=0.5), reads=[r_ps[bGc]], writes=[r_sg[1]])
                p.op("act", lambda: A.activation(out=sg[1][:, :T], in_=sg[1][:, :T], func=AF.Identity, scale=0.5, bias=0.5), reads=[r_sg[1]], writes=[r_sg[1]])
                p.op("dve", lambda: V.tensor_tensor(out=st[0][:, :T], in0=ps[bA][:, :T], in1=sg[0][:, :T], op=ALU.mult),
                     reads=[r_ps[bA], r_sg[0]], writes=[r_st[0]])
                p.op("dve", lambda: V.tensor_tensor(out=st[1][:, :T], in0=ps[bC][:, :T], in1=sg[1][:, :T], op=ALU.mult),
                     reads=[r_ps[bC], r_sg[1]], writes=[r_st[1]])
                p.op("dve", lambda: V.tensor_tensor(out=big[:, 16 + n, :T], in0=st[0][:, :T], in1=st[1][:, :T], op=ALU.add),
                     reads=[r_st[0], r_st[1]], writes=[r_big[16 + n]])
            yield None
        for n4 in range(2):
            (wv,), wr = wload((l, "wo", n4), [wcols(w_out[l], n4 * 512, 512)])
            for cc in range(4):
                n = 4 * n4 + cc
                b = galloc()
                proj(b, slice(0, T), 128, wv, cc * 128, [big[:, 16 + k, :T] for k in range(8)], r_big[16:24], wr, T)
                p.op("dve", lambda: V.tensor_tensor(out=xs_f[n], in0=xs_f[n], in1=ps[b][:, :T], op=ALU.add),
                     reads=[r_xT[n], r_ps[b]], writes=[r_xT[n]])
            yield None

        rms_rstd(l, xs_f, r_xT, D, T)
        for k in range(KC):
            p.op("dve", lambda: V.scalar_tensor_tensor(out=hs[k], in0=xs_f[k], scalar=vcol(l, V_NF + k), in1=rstd[:, :T],
                                                       op0=ALU.mult, op1=ALU.mult),
                 reads=[r_xT[k], r_rstd, r_vecs], writes=[r_hT[k]])
        ffw = {}

        def ffn_s1(j):
            i, cc, bs = j // 2, j % 2, j % 3
            if cc == 0:
                ffw["v"] = wload((l, "up", i), [wcols(w_up[l], i * 256, 256), wcols(w_up[l], cfg.DFF + i * 256, 256)])
            (wg_, wv_), wr = ffw["v"]
            bb = [galloc(), galloc()]
            p.op("pe", lambda: [mm(ps[bb[hf]][:, 0:T], wsrc[:, k, cc * 128:cc * 128 + 128], hs[k], k == 0, k == KC - 1)
                                for hf, wsrc in enumerate((wg_, wv_)) for k in range(KC)],
                 reads=[wr] + list(r_hT), writes=[r_ps[bb[0]], r_ps[bb[1]]])
            for half, wsrc in enumerate((wg_, wv_)):
                jj = j + half * NFF
                ub, r_ub = upb[2 * bs + half], r_upb[2 * bs + half]
                acc, r_acc = ybuf[:, 2 * bs + half, :T], r_ybuf[2 * bs + half]
                b = bb[half]
                p.op("act", lambda: [A.copy(out=ub[:, 2:2 + T], in_=ps[b][:, :T]),
                                     A.copy(out=ub[:, 0:2], in_=fh[:, jj, :])],
                     reads=[r_ps[b], r_fh[jj], r_fhall], writes=[r_ub])
                p.op("act", lambda: A.activation(out=acc, in_=ps[b][:, :T], func=AF.Identity,
                                                 scale=vcol(l, V_FW + 2 * 44 + jj), bias=vcol(l, V_FB + jj)),
                     reads=[r_ps[b], r_vecs], writes=[r_acc])
                p.op("act", lambda: A.copy(out=fh[:, jj, :], in_=ub[:, T:T + 2]), reads=[r_ub], writes=[r_fh[jj]])

        def ffn_s2a(j):
            bs = j % 3
            for kk in (1, 0):
                for half in range(2):
                    jj = j + half * NFF
                    ub, r_ub = upb[2 * bs + half], r_upb[2 * bs + half]
                    acc, r_acc = ybuf[:, 2 * bs + half, :T], r_ybuf[2 * bs + half]
                    p.op("dve", lambda: V.scalar_tensor_tensor(out=acc, in0=ub[:, kk:kk + T], scalar=vcol(l, V_FW + kk * 44 + jj),
                                                               in1=acc, op0=ALU.mult, op1=ALU.add),
                         reads=[r_ub, r_acc], writes=[r_acc])

        def ffn_s2b(j):
            bs = j % 3
            p.op("act", lambda: A.activation(out=sg[bs][:, :T], in_=ybuf[:, 2 * bs, :T], func=AF.Silu),
                 reads=[r_ybuf[2 * bs]], writes=[r_sg[bs]])

        def ffn_s2c(j):
            bs = j % 3
            p.op("dve", lambda: V.tensor_tensor(out=big[:, j, :T], in0=sg[bs][:, :T], in1=ybuf[:, 2 * bs + 1, :T], op=ALU.mult),
                 reads=[r_sg[bs], r_ybuf[2 * bs + 1]], writes=[r_big[j]])

        for it in range(NFF + 2):
            if it < NFF:
                ffn_s1(it)
            if 0 <= it - 1 < NFF:
                ffn_s2a(it - 1)
                ffn_s2b(it - 1)
            if 0 <= it - 2 < NFF:
                ffn_s2c(it - 2)
            if it % 2 == 1 or it >= NFF:
                yield None
        if last_tile:
            o_fst = (o_fst_p if prompt else o_fst_s)[l]
            p.dma("pool", r_fhall, o_fst.rearrange("(k p) s -> p k s", p=128), fh[:], reads=r_fh + [r_fhall])
        prod = [big[:, k, :T] for k in range(NFF)]
        for n in range(KC):
            (wv,), wr = wload((l, "dn", n), [wcols(w_down[l], n * 128, 128)])
            b = galloc()
            proj(b, slice(0, T), 128, wv, 0, prod, r_big[0:NFF], wr, T)
            p.op("dve", lambda: V.tensor_tensor(out=xs_f[n], in0=xs_f[n], in1=ps[b][:, :T], op=ALU.add),
                 reads=[r_xT[n], r_ps[b]], writes=[r_xT[n]])
            yield None

        if l < L - 1:
            if prompt:
                p.dma("pool", r_xTall, xmid.rearrange("(k p) s -> p k s", p=128)[:, :, tok0:tok0 + T], xT[:, :, :T],
                      reads=r_xT + [r_xTall], writes=[r_xmid[t]])
        else:
            rms_rstd(l, xs_f, r_xT, D, T)
            for k in range(KC):
                p.op("dve", lambda: V.scalar_tensor_tensor(out=ybuf[:, k, :T], in0=xs_f[k], scalar=vcol(l, V_FIN + k), in1=rstd[:, :T],
                                                           op0=ALU.mult, op1=ALU.mult),
                     reads=[r_xT[k], r_rstd, r_vecs], writes=[r_ybuf[k]])
            yT = yT_p if prompt else yT_s
            p.dma("pool", r_ybuf[0], yT.rearrange("(k p) s -> p k s", p=128)[:, :, tok0:tok0 + T], ybuf[:, :, :T], reads=r_ybuf)

    r_xmid = [Res("xmid%d" % i) for i in range(NT)]
    xs_keep = sb("xs_keep", [128, KC, TS], F32)
    r_xskeep = Res("xskeep")
    for l in range(L):
        p.op("dve", lambda: V.memset(ubuf[:, :, 0:HW_], 0.0), writes=r_ubuf)
        p.op("dve", lambda: V.memset(fh[:], 0.0), writes=r_fh + [r_fhall])
        def advance(g, until):
            for v in g:
                if v == until:
                    return True
            return False

        def step(g, until):
            try:
                v = next(g)
            except StopIteration:
                return "end"
            return "hit" if v == until else "run"

        cur = tile_pass(l, 0, "p")
        advance(cur, "SEG2")
        for t in range(NT):
            advance(cur, "SEG3")
            nxt = tile_pass(l, t + 1, "p") if t + 1 < NT else None
            a_live, b_live = True, nxt is not None
            while a_live or b_live:
                if a_live:
                    for _ in range(2):
                        if step(cur, None) == "end":
                            a_live = False
                            break
                if b_live:
                    if step(nxt, "SEG2") in ("hit", "end"):
                        b_live = False
            cur = nxt
        allc = r_ckvT + r_ckvK + r_kpeT + [r_cacheall]
        p.dmas("pool", r_cacheall,
               [(ckvT[:, :, 0:PAST], cacheT[l].rearrange("(k p) s -> p k s", p=128)),
                (ckvK[:, 0:PAST // 128, 0:256], cache_tok[l].rearrange("(kt p) c -> p kt c", p=128)),
                (kpeT[0:64, 0:PAST], kcacheT[l])], writes=allc)
        p.dma("pool", r_ubuf[0], ubuf[:, :, 0:HW_], sconvT[l].rearrange("(k p) s -> p k s", p=128), writes=r_ubuf)
        p.dma("pool", r_fhall, fh[:], sffnT[l].rearrange("(k p) s -> p k s", p=128), writes=r_fh + [r_fhall])
        for _ in tile_pass(l, 0, "s"):
            pass
        if l < L - 1:
            p.op("act", lambda: A.copy(out=xs_keep[:, :, :], in_=xT2[0][:, :, :TS]), reads=r_xT2[0] + [r_xTall2[0]], writes=[r_xskeep])
    p.finish("sp")
    return nc, p


def _rope_tables(cfg, pos):
    half = cfg.ROPE // 2
    inv = np.power(np.float32(cfg.THETA), -np.arange(half, dtype=np.float32) / np.float32(half)).astype(np.float32)
    ang = (pos.astype(np.float32)[:, None] * inv[None, :]).astype(np.float32)
    c = np.cos(ang).astype(np.float32).T
    s = np.sin(ang).astype(np.float32).T
    cos2 = np.ascontiguousarray(np.concatenate([c, c], axis=0))
    sin2 = np.ascontiguousarray(np.concatenate([s, -s], axis=0))
    return cos2, sin2


def _cols(v):
    return np.ascontiguousarray(v.reshape(-1, 128).T)


def _pack_vecs(cfg, l, I):
    out = np.zeros((128, NV), np.float32)
    out[:, V_NM:V_NM + 8] = _cols(I["norm_mix"][l])
    out[:, V_QN:V_QN + 3] = _cols(I["q_norm"][l])
    out[:, V_KN:V_KN + 2] = _cols(I["kv_norm"][l])
    out[:, V_CB:V_CB + 8] = _cols(I["conv_dw_b"][l])
    out[:, V_LG:V_LG + 8] = _cols(I["conv_ln_g"][l])
    out[:, V_LB:V_LB + 8] = _cols(I["conv_ln_b"][l])
    out[:, V_NF:V_NF + 8] = _cols(I["norm_ffn"][l])
    out[:, V_FB:V_FB + 44] = _cols(I["ffn_dw_b"][l])
    out[:, V_CW:V_CW + 248] = I["conv_dw"][l].reshape(31, 8, 128).transpose(2, 0, 1).reshape(128, 248)
    out[:, V_FW:V_FW + 132] = I["ffn_dw"][l].reshape(3, 44, 128).transpose(2, 0, 1).reshape(128, 132)
    out[:, V_FIN:V_FIN + 8] = _cols(I["norm_final"])
    return out


_CACHE = {}


def kernel(_cfg=None, **inputs):
    cfg = _cfg or Cfg
    I = {k: np.asarray(v) for k, v in inputs.items()}
    L, NC = cfg.DEPTH, cfg.NCORES
    f32 = np.float32
    key = (cfg.SEQ, cfg.PAST, cfg.TP)
    if key not in _CACHE:
        _CACHE[key] = build_program(cfg)
    nc, _ = _CACHE[key]

    cos_p, sin_p = _rope_tables(cfg, np.arange(cfg.SEQ))
    cos_s, sin_s = _rope_tables(cfg, cfg.PAST + np.arange(cfg.DEC_SEQ))
    vecs = np.ascontiguousarray(np.stack([_pack_vecs(cfg, l, I) for l in range(L)], axis=1))
    w_kvb = I["w_kvb"]
    w_ukT = np.ascontiguousarray(
        w_kvb.reshape(L, cfg.KVL, cfg.NH, 256)[:, :, :, :128].transpose(0, 3, 2, 1).reshape(L, 128, cfg.NH * 256))
    shared = {
        "vecs": vecs, "w_in": I["w_in"], "w_qb": I["w_qb"], "w_kvb": w_kvb, "w_ukT": w_ukT,
        "w_o_attn": I["w_o_attn"], "w_conv_out": I["w_conv_out"], "w_out": I["w_out"],
        "w_up": I["w_up"], "w_down": I["w_down"],
        "cos_p": cos_p, "sin_p": sin_p, "cos_s": cos_s, "sin_s": sin_s,
        "ident": np.eye(128, dtype=f32),
    }
    in_maps = []
    for b in range(NC):
        m = dict(shared)
        m["xT_p"] = np.ascontiguousarray(I["x_prompt"][b].T)
        m["xT_s"] = np.ascontiguousarray(I["x_sample"][b].T)
        m["cacheT"] = np.ascontiguousarray(I["cache_kv_latent"][:, b].transpose(0, 2, 1))
        m["cache_tok"] = np.ascontiguousarray(I["cache_kv_latent"][:, b])
        m["kcacheT"] = np.ascontiguousarray(I["cache_k_rope"][:, b].transpose(0, 2, 1))
        m["sconvT"] = np.ascontiguousarray(I["state_conv"][:, b].transpose(0, 2, 1))
        m["sffnT"] = np.ascontiguousarray(I["state_ffn_conv"][:, b].transpose(0, 2, 1))
        in_maps.append(m)
    res = run_bass_kernel_spmd(nc, in_maps, core_ids=list(range(NC)))
    R = res.results

    def gather(name, tr):
        return np.ascontiguousarray(np.stack([np.asarray(R[b][name]).transpose(*tr) for b in range(NC)], axis=0))

    y_p = gather("yT_p", (1, 0))
    y_s = gather("yT_s", (1, 0))

    def gl(name):
        return np.ascontiguousarray(np.stack([np.asarray(R[b][name]).transpose(0, 2, 1) for b in range(NC)], axis=1))

    return (y_p.astype(f32), y_s.astype(f32),
            gl("o_ckv_p"), gl("o_kpe_p"), gl("o_cst_p"), gl("o_fst_p"),
            gl("o_ckv_s"), gl("o_kpe_s"), gl("o_cst_s"), gl("o_fst_s"))
```

```python
import numpy as np
import concourse.bass as bass
import concourse.mybir as mybir
from concourse.bass_utils import run_bass_kernel_spmd

F32 = mybir.dt.float32
BF16 = mybir.dt.bfloat16
AF = mybir.ActivationFunctionType
ALU = mybir.AluOpType


class Cfg:
    D = 1024
    SEQ = 8192
    DEPTH = 2
    DEC_SEQ = 16
    PAST = 2048
    NH = 8
    NOPE = 128
    ROPE = 64
    VD = 128
    QL = 384
    KVL = 256
    CONV_W = 31
    DFF = 2816
    FW = 3
    EPS = 1e-6
    THETA = 10000.0
    TP = 256
    NCORES = 8


IN_COLS = 384 + 256 + 64 + 2048 + 2048
C_QA, C_KVA, C_KPE, C_CA, C_CG, C_GA, C_GC = 0, 384, 640, 704, 1728, 2752, 3776

V_NM, V_QN, V_KN, V_CB, V_LG, V_LB, V_NF, V_FB, V_CW, V_FW, V_FIN = 0, 8, 11, 13, 21, 29, 37, 45, 89, 337, 469
NV = 477


class Res:
    __slots__ = ("name", "w", "r", "dsem")

    def __init__(self, name):
        self.name = name
        self.w = None
        self.r = []
        self.dsem = None


class Prog:
    def __init__(self, nc):
        self.nc = nc
        self.eng = {"pe": nc.tensor, "act": nc.scalar, "dve": nc.vector, "pool": nc.gpsimd, "sp": nc.sync}
        self.sem = {}
        self.cnt = {}
        self.known = {e: {} for e in self.eng}
        self._keep = []
        for e in self.eng:
            self._mksem(e)
        self.nwait = 0
        self.nops = 0

    def _mksem(self, name):
        s = self.nc.semaphore("s_" + name)
        self.sem[name] = s.__enter__()
        self._keep.append(s)
        self.cnt[name] = 0

    def _deps(self, reads, writes):
        evs = {}
        for r in reads:
            ev = r.w
            if ev is not None and evs.get(ev[0], 0) < ev[1]:
                evs[ev[0]] = ev[1]
        for w in writes:
            ev = w.w
            if ev is not None and evs.get(ev[0], 0) < ev[1]:
                evs[ev[0]] = ev[1]
            for ev in w.r:
                if evs.get(ev[0], 0) < ev[1]:
                    evs[ev[0]] = ev[1]
        return evs

    def _wait(self, e, evs):
        kn = self.known[e]
        for k, v in evs.items():
            if e == "pe" and k == "pe":
                continue
            if kn.get(k, 0) >= v:
                continue
            self.eng[e].wait_ge(self.sem[k], v)
            self.nwait += 1
            kn[k] = v

    def _commit(self, ev, reads, writes):
        for r in reads:
            k = ev[0]
            r.r = [x for x in r.r if x[0] != k]
            r.r.append(ev)
        for w in writes:
            w.w = ev
            w.r = []

    def op(self, e, fn, reads=(), writes=()):
        self._wait(e, self._deps(reads, writes))
        inst = fn()
        if isinstance(inst, (list, tuple)):
            inst = inst[-1]
        self.cnt[e] += 1
        inst.then_inc(self.sem[e], 1)
        self._commit((e, self.cnt[e]), reads, writes)
        self.nops += 1

    def dma(self, q, sres, out, in_, reads=(), writes=()):
        self.dmas(q, sres, [(out, in_)], reads, writes)

    def dmas(self, q, sres, pairs, reads=(), writes=()):
        if sres.dsem is None:
            sres.dsem = "d%d_%s" % (len(self.sem), sres.name)
            self._mksem(sres.dsem)
        self._wait(q, self._deps(reads, writes))
        for out, in_ in pairs:
            inst = self.eng[q].dma_start(out=out, in_=in_)
            self.cnt[sres.dsem] += 16
            inst.then_inc(self.sem[sres.dsem], 16)
            self.nops += 1
        self._commit((sres.dsem, self.cnt[sres.dsem]), reads, writes)

    def finish(self, e="sp"):
        evs = {k: v for k, v in self.cnt.items() if v > 0}
        self._wait(e, evs)


def build_program(cfg):
    nc = bass.Bass("TRN2", target_bir_lowering=False)
    D, SEQ, L, TS, PAST, NH = cfg.D, cfg.SEQ, cfg.DEPTH, cfg.DEC_SEQ, cfg.PAST, cfg.NH
    TP = cfg.TP
    KC = D // 128
    NFF = cfg.DFF // 128
    NT = SEQ // TP
    CW = cfg.CONV_W
    HW_ = CW - 1
    SCALE = float((cfg.NOPE + cfg.ROPE) ** -0.5)
    EPS = cfg.EPS
    KEYMAX = max(SEQ, PAST + 128)
    NKT = KEYMAX // 128

    def din(name, shape, dt=F32):
        return nc.dram_tensor(name, list(shape), dt, kind="ExternalInput").ap()

    def dout(name, shape, dt=F32):
        return nc.dram_tensor(name, list(shape), dt, kind="ExternalOutput").ap()

    xT_p = din("xT_p", [D, SEQ])
    xT_s = din("xT_s", [D, TS])
    cacheT = din("cacheT", [L, cfg.KVL, PAST])
    cache_tok = din("cache_tok", [L, PAST, cfg.KVL])
    kcacheT = din("kcacheT", [L, cfg.ROPE, PAST])
    sconvT = din("sconvT", [L, D, HW_])
    sffnT = din("sffnT", [L, 2 * cfg.DFF, 2])
    vecs_d = din("vecs", [128, L, NV])
    w_in = din("w_in", [L, D, IN_COLS])
    w_qb = din("w_qb", [L, cfg.QL, NH * 192])
    w_kvb = din("w_kvb", [L, cfg.KVL, NH * 256])
    w_ukT_d = din("w_ukT", [L, 128, NH * 256])
    w_oa = din("w_o_attn", [L, D, D])
    w_co = din("w_conv_out", [L, D, D])
    w_out = din("w_out", [L, D, D])
    w_up = din("w_up", [L, D, 2 * cfg.DFF])
    w_down = din("w_down", [L, cfg.DFF, D])
    cos_p = din("cos_p", [64, SEQ])
    sin_p = din("sin_p", [64, SEQ])
    cos_s = din("cos_s", [64, TS])
    sin_s = din("sin_s", [64, TS])
    ident_d = din("ident", [128, 128])

    yT_p = dout("yT_p", [D, SEQ])
    yT_s = dout("yT_s", [D, TS])
    o_ckv_p = dout("o_ckv_p", [L, cfg.KVL, SEQ])
    o_kpe_p = dout("o_kpe_p", [L, cfg.ROPE, SEQ])
    o_cst_p = dout("o_cst_p", [L, D, HW_])
    o_fst_p = dout("o_fst_p", [L, 2 * cfg.DFF, 2])
    o_ckv_s = dout("o_ckv_s", [L, cfg.KVL, TS])
    o_kpe_s = dout("o_kpe_s", [L, cfg.ROPE, TS])
    o_cst_s = dout("o_cst_s", [L, D, HW_])
    o_fst_s = dout("o_fst_s", [L, 2 * cfg.DFF, 2])
    xmid = nc.dram_tensor("xmid", [D, SEQ], F32, kind="Internal").ap()
    NPIECE = 60 * L
    wsc = nc.dram_tensor("wsc", [NPIECE, 128, 4096], BF16, kind="Internal").ap()

    p = Prog(nc)

    def sb(name, shape, dt):
        return nc.sbuf_tensor(name, list(shape), dt).__enter__()

    ckvT = sb("ckvT", [128, 2, KEYMAX], BF16)
    ckvK = sb("ckvK", [128, NKT, 260], BF16)
    kpeT = sb("kpeT", [128, KEYMAX], BF16)
    r_ckvT = [Res("ckvT%d" % i) for i in range(NKT)]
    r_ckvK = [Res("ckvK%d" % i) for i in range(NKT)]
    r_kpeT = [Res("kpeT%d" % i) for i in range(NKT)]
    r_cacheall = Res("cacheall")

    vecs = sb("vecs_sb", [128, L, NV], F32)
    r_vecs = Res("vecs")
    ident = sb("ident_sb", [128, 128], F32)
    r_ident = Res("ident")
    ident_b = sb("ident_b", [128, 128], BF16)
    r_identb = Res("identb")
    otok = sb("otok", [128, 2, 256], BF16)
    r_otok = Res("otok")
    rdn = sb("rdn", [128, 2], F32)
    r_rdn = [Res("rdn0"), Res("rdn1")]
    ones_f = sb("ones_f", [128, 128], F32)
    r_ones = Res("ones")
    wukh = [sb("wukh%d" % i, [128, 256], BF16) for i in range(2)]
    r_wukh = [Res("wukh%d" % i) for i in range(2)]
    wuvh = [sb("wuvh%d" % i, [128, 256], BF16) for i in range(2)]
    r_wuvh = [Res("wuvh%d" % i) for i in range(2)]

    xT2 = [sb("xT_%d" % i, [128, KC, TP], F32) for i in range(2)]
    r_xT2 = [[Res("xT%d_%d" % (j, i)) for i in range(KC)] for j in range(2)]
    r_xTall2 = [Res("xTall%d" % j) for j in range(2)]
    hT2 = [sb("hT_%d" % i, [128, KC, TP], BF16) for i in range(2)]
    r_hT2 = [[Res("hT%d_%d" % (j, i)) for i in range(KC)] for j in range(2)]
    qkv_s = sb("qkv_s", [128, 5, TP], F32)
    r_qkv = [Res("qkv%d" % i) for i in range(5)]
    big = sb("big", [128, 24, TP], BF16)
    r_big = [Res("big%d" % i) for i in range(24)]
    ubuf = sb("ubuf", [128, KC, HW_ + TP], F32)
    r_ubuf = [Res("ubuf%d" % i) for i in range(KC)]
    ybuf = sb("ybuf", [128, KC, TP], F32)
    r_ybuf = [Res("ybuf%d" % i) for i in range(KC)]
    sq = [sb("sq%d" % i, [128, TP], F32) for i in range(2)]
    r_sq = [Res("sq%d" % i) for i in range(2)]
    rstd = sb("rstd", [128, TP], F32)
    r_rstd = Res("rstd")
    qan = sb("qan", [128, 3, TP], BF16)
    r_qan = Res("qan")
    ckvf = sb("ckvf", [128, 2, TP], F32)
    r_ckvf = Res("ckvf")
    ra = sb("ra", [64, TP], F32)
    rb = sb("rb", [64, TP], F32)
    r_ra, r_rb = Res("ra"), Res("rb")
    kpo, r_kpo = ra, r_ra
    cosT = sb("cosT", [64, TP], F32)
    sinT = sb("sinT", [64, TP], F32)
    r_rope = Res("rope")
    qn = [sb("qn%d" % i, [128, TP], BF16) for i in range(2)]
    r_qn = [Res("qn%d" % i) for i in range(2)]
    qlat = [sb("qlat%d" % i, [128, 2, TP], BF16) for i in range(2)]
    r_qlat = [Res("qlat%d" % i) for i in range(2)]
    qf = sb("qf", [64, TP], F32)
    r_qf = Res("qf")
    kpf, r_kpf = qf, r_qf
    qpe = [sb("qpe%d" % i, [128, TP], BF16) for i in range(2)]
    r_qpe = [Res("qpe%d" % i) for i in range(2)]
    PT = [sb("PT%d" % i, [128, TP], BF16) for i in range(3)]
    r_PT = [Res("PT%d" % i) for i in range(3)]
    olat = [sb("olat%d" % i, [128, 2, TP], BF16) for i in range(2)]
    r_olat = [Res("olat%d" % i) for i in range(2)]
    sg = [sb("sg%d" % i, [128, TP], F32) for i in range(3)]
    r_sg = [Res("sg%d" % i) for i in range(3)]
    upb = [sb("upb%d" % i, [128, TP + 2], F32) for i in range(6)]
    r_upb = [Res("upb%d" % i) for i in range(6)]
    st = [upb[i][:, 0:TP] for i in range(4)]
    r_st = r_upb[0:4]
    fh = sb("fh", [128, 2 * NFF, 2], F32)
    r_fh = [Res("fh%d" % i) for i in range(2 * NFF)]
    r_fhall = Res("fhall")
    WSLOT = 4096
    NSLOT = 4
    ws = [sb("ws%d" % i, [128, WSLOT], BF16) for i in range(NSLOT)]
    r_ws = [Res("ws%d" % i) for i in range(NSLOT)]

    ps = [nc.psum_tensor("ps%d" % i, [128, 512], F32).__enter__() for i in range(8)]
    r_ps = [Res("ps%d" % i) for i in range(8)]

    state = {"g": 0, "gset": list(range(8)), "ws": 0, "ev": 0}

    def galloc():
        gs = state["gset"]
        b = gs[state["g"] % len(gs)]
        state["g"] += 1
        return b

    V = nc.vector
    G = nc.gpsimd
    A = nc.scalar
    PE = nc.tensor

    def mm(out, lhsT, rhs, start, stop):
        return PE.matmul(out, lhsT=lhsT, rhs=rhs, start=start, stop=stop, skip_group_check=True)

    pieces = {}

    def cached_load(key, dst_flat, dst_res, pairs, total):
        if key not in pieces:
            idx = len(pieces)
            assert idx < NPIECE
            pr = Res("piece%d" % idx)
            pieces[key] = (idx, pr)
            p.dmas("pool", dst_res, pairs, writes=[dst_res])
            p.dma("sp", dst_res, wsc[idx, :, 0:total], dst_flat[:, 0:total], reads=[dst_res], writes=[pr])
        else:
            idx, pr = pieces[key]
            p.dma("sp", dst_res, dst_flat[:, 0:total], wsc[idx, :, 0:total], reads=[pr], writes=[dst_res])

    def wload(key, segs):
        i = state["ws"] % NSLOT
        state["ws"] += 1
        off = 0
        views = []
        pairs = []
        for src in segs:
            shp = src.shape
            K, n = shp[1], shp[2]
            v = ws[i][:, off:off + K * n].rearrange("p (k n) -> p k n", k=K)
            pairs.append((v, src))
            off += K * n
            views.append(v)
        assert off <= WSLOT
        cached_load(key, ws[i], r_ws[i], pairs, off)
        return views, r_ws[i]

    def wcols(w_l, c0, n):
        return w_l.rearrange("(k p) n -> p k n", p=128)[:, :, c0:c0 + n]

    p.dma("sp", r_vecs, vecs[:], vecs_d, writes=[r_vecs])
    p.dma("sp", r_ident, ident[:], ident_d, writes=[r_ident])
    p.op("dve", lambda: V.memset(ones_f[:], 1.0), writes=[r_ones])
    p.op("dve", lambda: V.memset(kpeT[64:128, :], 0.0), writes=r_kpeT + [r_cacheall])
    p.op("dve", lambda: V.memset(ckvK[:, :, 256:257], 1.0), writes=r_ckvK + [r_cacheall])
    p.op("dve", lambda: V.tensor_copy(out=ident_b[:], in_=ident[:]), reads=[r_ident], writes=[r_identb])
    for i in range(2):
        p.op("dve", lambda: V.memset(qpe[i][64:128, :], 0.0), writes=[r_qpe[i]])

    def vcol(l, c):
        return vecs[:, l, c:c + 1]

    def rms_rstd(l, srcs, src_res, Dn, T):
        b = galloc()
        n = len(srcs)
        for i, a in enumerate(srcs):
            s = i % 2
            p.op("act", lambda: A.activation(out=sq[s][:, :T], in_=a, func=AF.Square),
                 reads=[src_res[i]], writes=[r_sq[s]])
            p.op("pe", lambda: mm(ps[b][:, :T], ones_f[:], sq[s][:, :T], i == 0, i == n - 1),
                 reads=[r_sq[s], r_ones], writes=[r_ps[b]])
        p.op("act", lambda: A.activation(out=rstd[:, :T], in_=ps[b][:, :T], func=AF.Ln, scale=1.0 / Dn, bias=EPS),
             reads=[r_ps[b]], writes=[r_rstd])
        p.op("act", lambda: A.activation(out=rstd[:, :T], in_=rstd[:, :T], func=AF.Exp, scale=-0.5),
             reads=[r_rstd], writes=[r_rstd])

    def rope(src, r_src, dst, r_dst, T):
        p.op("dve", lambda: V.tensor_tensor(out=ra[:, :T], in0=src[:, :T], in1=cosT[:, :T], op=ALU.mult),
             reads=[r_src, r_rope], writes=[r_ra])
        p.op("dve", lambda: [V.tensor_tensor(out=rb[0:32, :T], in0=src[32:64, :T], in1=sinT[32:64, :T], op=ALU.mult),
                             V.tensor_tensor(out=rb[32:64, :T], in0=src[0:32, :T], in1=sinT[0:32, :T], op=ALU.mult)],
             reads=[r_src, r_rope], writes=[r_rb])
        p.op("dve", lambda: V.tensor_tensor(out=dst[0:64, :T], in0=ra[:, :T], in1=rb[:, :T], op=ALU.add),
             reads=[r_ra, r_rb], writes=[r_dst])

    def proj(bank, cols, M, wv, c0, xs, x_res, w_res, T, extra_reads=()):
        K = len(xs)
        p.op("pe", lambda: [mm(ps[bank][0:M, cols], wv[:, k, c0:c0 + M], xs[k], k == 0, k == K - 1) for k in range(K)],
             reads=[w_res] + list(x_res) + list(extra_reads), writes=[r_ps[bank]])

    def tile_pass(l, t, mode):
        prompt = mode == "p"
        T = TP if prompt else TS
        tok0 = t * T if prompt else 0
        kbase = tok0 if prompt else PAST
        last_tile = (t == NT - 1) if prompt else True
        bi = (t % 2) if prompt else 0
        xT, r_xT, r_xTall, hT, r_hT = xT2[bi], r_xT2[bi], r_xTall2[bi], hT2[bi], r_hT2[bi]
        xs_f = [xT[:, k, :T] for k in range(KC)]
        hs = [hT[:, k, :T] for k in range(KC)]
        W_in = w_in[l]

        if l == 0:
            src = (xT_p if prompt else xT_s).rearrange("(k p) s -> p k s", p=128)[:, :, tok0:tok0 + T]
            p.dma("pool", r_xTall, xT[:, :, :T], src, writes=r_xT + [r_xTall])
        elif prompt:
            src = xmid.rearrange("(k p) s -> p k s", p=128)[:, :, tok0:tok0 + T]
            p.dma("pool", r_xTall, xT[:, :, :T], src, reads=[r_xmid[t]], writes=r_xT + [r_xTall])
        else:
            p.op("act", lambda: A.copy(out=xT[:, :, :T], in_=xs_keep[:, :, :]), reads=[r_xskeep], writes=r_xT + [r_xTall])
        cs, sn = (cos_p, sin_p) if prompt else (cos_s, sin_s)
        p.dmas("pool", r_rope, [(cosT[:, :T], cs[:, tok0:tok0 + T]), (sinT[:, :T], sn[:, tok0:tok0 + T])], writes=[r_rope])

        rms_rstd(l, xs_f, r_xT, D, T)
        for k in range(KC):
            p.op("dve", lambda: V.scalar_tensor_tensor(out=hs[k], in0=xs_f[k], scalar=vcol(l, V_NM + k), in1=rstd[:, :T],
                                                       op0=ALU.mult, op1=ALU.mult),
                 reads=[r_xT[k], r_rstd, r_vecs], writes=[r_hT[k]])
        yield None

        (wv,), wr = wload((l, "qa"), [wcols(W_in, C_QA, 384)])
        qa = [qkv_s[:, j, :T] for j in range(3)]
        for j in range(3):
            b = galloc()
            proj(b, slice(0, T), 128, wv, j * 128, hs, r_hT, wr, T)
            p.op("act", lambda: A.copy(out=qa[j], in_=ps[b][:, :T]), reads=[r_ps[b]], writes=[r_qkv[j]])
        rms_rstd(l, qa, r_qkv[0:3], cfg.QL, T)
        for j in range(3):
            p.op("dve", lambda: V.scalar_tensor_tensor(out=qan[:, j, :T], in0=qa[j], scalar=vcol(l, V_QN + j), in1=rstd[:, :T],
                                                       op0=ALU.mult, op1=ALU.mult),
                 reads=[r_qkv[j], r_rstd, r_vecs], writes=[r_qan])
        yield None

        (wv,), wr = wload((l, "kva"), [wcols(W_in, C_KVA, 320)])
        kva = [qkv_s[:, 3 + j, :T] for j in range(2)]
        for j in range(2):
            b = galloc()
            proj(b, slice(0, T), 128, wv, j * 128, hs, r_hT, wr, T)
            p.op("act", lambda: A.copy(out=kva[j], in_=ps[b][:, :T]), reads=[r_ps[b]], writes=[r_qkv[3 + j]])
        b = galloc()
        proj(b, slice(0, T), 64, wv, 256, hs, r_hT, wr, T)
        p.op("act", lambda: A.copy(out=kpf[:, :T], in_=ps[b][0:64, :T]), reads=[r_ps[b]], writes=[r_kpf])
        yield None
        rms_rstd(l, kva, r_qkv[3:5], cfg.KVL, T)
        for j in range(2):
            p.op("dve", lambda: V.scalar_tensor_tensor(out=ckvf[:, j, :T], in0=kva[j], scalar=vcol(l, V_KN + j), in1=rstd[:, :T],
                                                       op0=ALU.mult, op1=ALU.mult),
                 reads=[r_qkv[3 + j], r_rstd, r_vecs], writes=[r_ckvf])
        yield None
        o_ckv = (o_ckv_p if prompt else o_ckv_s)[l]
        o_kpe = (o_kpe_p if prompt else o_kpe_s)[l]
        p.dma("pool", r_ckvf, o_ckv.rearrange("(k p) s -> p k s", p=128)[:, :, tok0:tok0 + T], ckvf[:, :, :T], reads=[r_ckvf])
        nkt_new = (T + 127) // 128
        kt0 = kbase // 128
        new_res = []
        for s in range(nkt_new):
            new_res += [r_ckvT[kt0 + s], r_ckvK[kt0 + s], r_kpeT[kt0 + s]]
        p.op("act", lambda: A.copy(out=ckvT[:, :, kbase:kbase + T], in_=ckvf[:, :, :T]), reads=[r_ckvf],
             writes=[r_ckvT[kt0 + s] for s in range(nkt_new)] + [r_cacheall])
        for s in range(nkt_new):
            n = min(128, T - s * 128)
            b = galloc()
            p.op("pe", lambda: [PE.transpose(out=ps[b][0:n, j * 128:(j + 1) * 128], in_=ckvf[:, j, s * 128:s * 128 + n], identity=ident[:])
                                for j in range(2)], reads=[r_ckvf, r_ident], writes=[r_ps[b]])
            p.op("act", lambda: A.copy(out=ckvK[0:n, kt0 + s, 0:256], in_=ps[b][0:n, 0:256]), reads=[r_ps[b]],
                 writes=[r_ckvK[kt0 + s], r_cacheall])
        yield None
        rope(kpf, r_kpf, kpo, r_kpo, T)
        p.dma("pool", r_kpo, o_kpe[:, tok0:tok0 + T], kpo[:, :T], reads=[r_kpo])
        p.op("act", lambda: A.copy(out=kpeT[0:64, kbase:kbase + T], in_=kpo[:, :T]), reads=[r_kpo],
             writes=[r_kpeT[kt0 + s] for s in range(nkt_new)] + [r_cacheall])

        yield None
        taps = []

        def drain(n):
            for _ in range(min(n, len(taps))):
                taps.pop(0)()

        for i in range(KC // 2):
            (wa, wg), wr = wload((l, "cv", i), [wcols(W_in, C_CA + 256 * i, 256), wcols(W_in, C_CG + 256 * i, 256)])
            for cc in range(2):
                c = 2 * i + cc
                bA = galloc()
                proj(bA, slice(0, T), 128, wa, cc * 128, hs, r_hT, wr, T)
                bG = galloc()
                proj(bG, slice(0, T), 128, wg, cc * 128, hs, r_hT, wr, T)
                p.op("act", lambda: A.activation(out=qkv_s[:, cc, :T], in_=ps[bG][:, :T], func=AF.Tanh, scale=0.5), reads=[r_ps[bG]], writes=[r_qkv[cc]])
                p.op("act", lambda: A.activation(out=qkv_s[:, cc, :T], in_=qkv_s[:, cc, :T], func=AF.Identity, scale=0.5, bias=0.5), reads=[r_qkv[cc]], writes=[r_qkv[cc]])
                if prompt and t > 0:
                    p.op("dve", lambda: V.tensor_copy(out=ubuf[:, c, 0:HW_], in_=ubuf[:, c, TP:TP + HW_]),
                         reads=[r_ubuf[c]], writes=[r_ubuf[c]])
                p.op("dve", lambda: V.tensor_tensor(out=ubuf[:, c, HW_:HW_ + T], in0=ps[bA][:, :T], in1=qkv_s[:, cc, :T], op=ALU.mult),
                     reads=[r_ps[bA], r_qkv[cc]], writes=[r_ubuf[c]])
            def mk_tap(k, c):
                wk = vcol(l, V_CW + k * 8 + c)
                if k == 0:
                    return lambda: p.op("dve", lambda: V.tensor_scalar(out=ybuf[:, c, :T], in0=ubuf[:, c, 0:T], scalar1=wk,
                                                                       scalar2=vcol(l, V_CB + c), op0=ALU.mult, op1=ALU.add),
                                        reads=[r_ubuf[c], r_vecs], writes=[r_ybuf[c]])
                return lambda: p.op("dve", lambda: V.scalar_tensor_tensor(out=ybuf[:, c, :T], in0=ubuf[:, c, k:k + T], scalar=wk,
                                                                          in1=ybuf[:, c, :T], op0=ALU.mult, op1=ALU.add),
                                    reads=[r_ubuf[c], r_ybuf[c]], writes=[r_ybuf[c]])
            for k in range(CW):
                for cc in range(2):
                    taps.append(mk_tap(k, 2 * i + cc))
            yield None
        yield "SEG2"
        state["gset"] = [7]
        S_B = [0, 1, 6]
        ACC = [(2, 3), (4, 5)]
        if prompt:
            nkt = (tok0 + T) // 128
            ktiles = [(kt, 128) for kt in range(nkt)]
            diag0 = tok0 // 128
        else:
            ktiles = [(kt, 128) for kt in range(PAST // 128)] + [(PAST // 128, TS)]
            diag0 = 10 ** 9
        wq_views = {}
        qan_s = [qan[:, j, :T] for j in range(3)]

        def prepA(h):
            hl = h % 4
            if hl == 0:
                (wvq,), wrq = wload((l, "qb", h), [wcols(w_qb[l], h * 192, 768)])
                wq_views["v"] = (wvq, wrq)
            wvq, wrq = wq_views["v"]
            i2 = h % 2
            cached_load((l, "uk", h), wukh[i2], r_wukh[i2], [(wukh[i2][:, :], w_ukT_d[l][:, h * 256:(h + 1) * 256])], 256)
            cached_load((l, "uv", h), wuvh[i2], r_wuvh[i2],
                        [(wuvh[i2][:, k * 128:(k + 1) * 128], w_kvb[l][k * 128:(k + 1) * 128, h * 256 + 128:h * 256 + 256])
                         for k in range(2)], 256)
            b = galloc()
            proj(b, slice(0, T), 128, wvq, hl * 192, qan_s, [r_qan], wrq, T)
            p.op("act", lambda: A.copy(out=qn[i2][:, :T], in_=ps[b][:, :T]), reads=[r_ps[b]], writes=[r_qn[i2]])
            b = galloc()
            proj(b, slice(0, T), 64, wvq, hl * 192 + 128, qan_s, [r_qan], wrq, T)
            p.op("act", lambda: A.copy(out=qf[:, :T], in_=ps[b][0:64, :T]), reads=[r_ps[b]], writes=[r_qf])

        def prepB(h):
            i2 = h % 2
            b = galloc()
            p.op("pe", lambda: [mm(ps[b][:, j * T:(j + 1) * T], wukh[i2][:, j * 128:(j + 1) * 128], qn[i2][:, :T], True, True)
                                for j in range(2)], reads=[r_qn[i2], r_wukh[i2]], writes=[r_ps[b]])
            p.op("act", lambda: A.copy(out=qlat[i2][:, :, :T], in_=ps[b][:, 0:2 * T].rearrange("p (j t) -> p j t", j=2)),
                 reads=[r_ps[b]], writes=[r_qlat[i2]])
            rope(qf, r_qf, qpe[i2], r_qpe[i2], T)

        def score(h, idx):
            kt, nk = ktiles[idx]
            i2 = h % 2
            sb_ = S_B[idx % 3]
            j = kt - diag0
            c0 = 128 * j if j > 0 else 0
            k0 = kt * 128
            p.op("pe", lambda: [mm(ps[sb_][0:nk, c0:T], ckvT[:, 0, k0:k0 + nk], qlat[i2][:, 0, c0:T], True, False),
                                mm(ps[sb_][0:nk, c0:T], ckvT[:, 1, k0:k0 + nk], qlat[i2][:, 1, c0:T], False, False),
                                mm(ps[sb_][0:nk, c0:T], kpeT[:, k0:k0 + nk], qpe[i2][:, c0:T], False, True)],
                 reads=[r_ckvT[kt], r_kpeT[kt], r_qlat[i2], r_qpe[i2]], writes=[r_ps[sb_]])
            pi = idx % 3
            p.op("act", lambda: A.activation(out=PT[pi][0:nk, c0:T], in_=ps[sb_][0:nk, c0:T], func=AF.Exp, scale=SCALE),
                 reads=[r_ps[sb_]], writes=[r_PT[pi]])
            if j >= 0:
                p.op("pool", lambda: nc.gpsimd.memset(PT[pi][64:128, c0:c0 + 64], 0.0), writes=[r_PT[pi]])

        NQC = (T + 127) // 128

        def pv(h, idx):
            kt, nk = ktiles[idx]
            j = kt - diag0
            pi = idx % 3
            first = idx == 0
            banks = ACC[h % 2]
            mms = []
            for qc in range(NQC):
                if j > 0 and qc < j:
                    continue
                qn = min(128, T - qc * 128)
                mms.append((qc, qn))
            p.op("pe", lambda: [mm(ps[banks[qc]][0:qn, 0:257], PT[pi][0:nk, qc * 128:qc * 128 + qn], ckvK[0:nk, kt, 0:257], first, True)
                                for qc, qn in mms],
                 reads=[r_ckvK[kt], r_PT[pi]], writes=[r_ps[banks[qc]] for qc, _ in mms])

        def finish_head(h):
            i2 = h % 2
            banks = ACC[h % 2]
            for qc in range(NQC):
                qn = min(128, T - qc * 128)
                bq = banks[qc]
                p.op("dve", lambda: V.reciprocal(out=rdn[0:qn, qc:qc + 1], in_=ps[bq][0:qn, 256:257]), reads=[r_ps[bq]], writes=[r_rdn[qc]])
                p.op("act", lambda: A.activation(out=otok[0:qn, qc, :], in_=ps[bq][0:qn, 0:256], func=AF.Copy, scale=rdn[0:qn, qc:qc + 1]),
                     reads=[r_ps[bq], r_rdn[qc]], writes=[r_otok])

        def tr_stage(h):
            i2 = h % 2
            b = galloc()
            pv_ = ps[b][:, 0:256].bitcast(BF16)
            p.op("pe", lambda: [PE.transpose(out=pv_[:, cj * T + qc * 128:cj * T + qc * 128 + min(128, T - qc * 128)],
                                             in_=otok[0:min(128, T - qc * 128), qc, cj * 128:(cj + 1) * 128],
                                             identity=ident_b[0:min(128, T - qc * 128), 0:min(128, T - qc * 128)])
                                for cj in range(2) for qc in range(NQC)],
                 reads=[r_otok, r_identb], writes=[r_ps[b]])
            p.op("act", lambda: A.copy(out=olat[i2][:, :, :T], in_=pv_[:, 0:2 * T].rearrange("p (j t) -> p j t", j=2)),
                 reads=[r_ps[b]], writes=[r_olat[i2]])

        def wuv_stage(h):
            i2 = h % 2
            b = galloc()
            p.op("pe", lambda: [mm(ps[b][:, :T], wuvh[i2][:, 0:128], olat[i2][:, 0, :T], True, False),
                                mm(ps[b][:, :T], wuvh[i2][:, 128:256], olat[i2][:, 1, :T], False, True)],
                 reads=[r_wuvh[i2], r_olat[i2]], writes=[r_ps[b]])
            p.op("act", lambda: A.copy(out=big[:, h, :T], in_=ps[b][:, :T]), reads=[r_ps[b]], writes=[r_big[h]])

        prepA(0)
        prepB(0)
        nk_ = len(ktiles)
        quota = -(-len(taps) // (NH * nk_))
        i_t, i_c, i_a, i_b = min(1, nk_ - 1), min(3, nk_ - 1), min(4, nk_ - 1), min(6, nk_ - 1)
        for h in range(NH):
            score(h, 0)
            if nk_ > 1:
                score(h, 1)
            for idx in range(nk_):
                if idx + 2 < nk_:
                    score(h, idx + 2)
                if idx == i_t and h > 0:
                    tr_stage(h - 1)
                if idx == i_c and h > 0:
                    wuv_stage(h - 1)
                if idx == i_a and h + 1 < NH:
                    prepA(h + 1)
                if idx == i_b and h + 1 < NH:
                    prepB(h + 1)
                pv(h, idx)
                drain(quota)
            finish_head(h)
        tr_stage(NH - 1)
        wuv_stage(NH - 1)
        state["gset"] = list(range(8))

        drain(10 ** 6)
        yield "SEG3"
        if last_tile:
            o_cst = (o_cst_p if prompt else o_cst_s)[l]
            p.dma("pool", r_ubuf[0], o_cst.rearrange("(k p) s -> p k s", p=128), ubuf[:, :, T:T + HW_], reads=r_ubuf)
        b1 = galloc()
        b2 = galloc()
        for c in range(KC):
            s = c % 2
            p.op("pe", lambda: mm(ps[b1][:, :T], ones_f[:], ybuf[:, c, :T], c == 0, c == KC - 1),
                 reads=[r_ybuf[c], r_ones], writes=[r_ps[b1]])
            p.op("act", lambda: A.activation(out=sq[s][:, :T], in_=ybuf[:, c, :T], func=AF.Square),
                 reads=[r_ybuf[c]], writes=[r_sq[s]])
            p.op("pe", lambda: mm(ps[b2][:, :T], ones_f[:], sq[s][:, :T], c == 0, c == KC - 1),
                 reads=[r_sq[s], r_ones], writes=[r_ps[b2]])
        mean, msq, var, nmr = st[0], st[1], st[2], st[3]
        p.op("dve", lambda: V.tensor_scalar(out=mean[:, :T], in0=ps[b1][:, :T], scalar1=1.0 / D, scalar2=None, op0=ALU.mult),
             reads=[r_ps[b1]], writes=[r_st[0]])
        p.op("dve", lambda: V.tensor_tensor(out=msq[:, :T], in0=mean[:, :T], in1=mean[:, :T], op=ALU.mult),
             reads=[r_st[0]], writes=[r_st[1]])
        p.op("dve", lambda: V.scalar_tensor_tensor(out=var[:, :T], in0=ps[b2][:, :T], scalar=1.0 / D, in1=msq[:, :T],
                                                   op0=ALU.mult, op1=ALU.subtract), reads=[r_ps[b2], r_st[1]], writes=[r_st[2]])
        lrs, r_lrs = st[1], r_st[1]
        p.op("act", lambda: A.activation(out=lrs[:, :T], in_=var[:, :T], func=AF.Ln, bias=EPS), reads=[r_st[2]], writes=[r_lrs])
        p.op("act", lambda: A.activation(out=lrs[:, :T], in_=lrs[:, :T], func=AF.Exp, scale=-0.5), reads=[r_lrs], writes=[r_lrs])
        p.op("dve", lambda: V.scalar_tensor_tensor(out=nmr[:, :T], in0=mean[:, :T], scalar=-1.0, in1=lrs[:, :T],
                                                   op0=ALU.mult, op1=ALU.mult), reads=[r_st[0], r_lrs], writes=[r_st[3]])
        for c in range(KC):
            p.op("dve", lambda: V.tensor_tensor(out=ybuf[:, c, :T], in0=ybuf[:, c, :T], in1=lrs[:, :T], op=ALU.mult),
                 reads=[r_ybuf[c], r_lrs], writes=[r_ybuf[c]])
            p.op("dve", lambda: V.tensor_tensor(out=ybuf[:, c, :T], in0=ybuf[:, c, :T], in1=nmr[:, :T], op=ALU.add),
                 reads=[r_ybuf[c], r_st[3]], writes=[r_ybuf[c]])
            p.op("act", lambda: A.activation(out=big[:, 8 + c, :T], in_=ybuf[:, c, :T], func=AF.Silu,
                                             scale=vcol(l, V_LG + c), bias=vcol(l, V_LB + c)),
                 reads=[r_ybuf[c], r_vecs], writes=[r_big[8 + c]])
            yield None

        for n2 in range(KC // 2):
            (wo, wc), wr1 = wload((l, "oc", n2), [wcols(w_oa[l], n2 * 256, 256), wcols(w_co[l], n2 * 256, 256)])
            (wga, wgc), wr2 = wload((l, "gt", n2), [wcols(W_in, C_GA + n2 * 256, 256), wcols(W_in, C_GC + n2 * 256, 256)])
            for cc in range(2):
                n = 2 * n2 + cc
                bA = galloc()
                proj(bA, slice(0, T), 128, wo, cc * 128, [big[:, h, :T] for h in range(8)], r_big[0:8], wr1, T)
                bC = galloc()
                proj(bC, slice(0, T), 128, wc, cc * 128, [big[:, 8 + c, :T] for c in range(8)], r_big[8:16], wr1, T)
                bGa = galloc()
                proj(bGa, slice(0, T), 128, wga, cc * 128, hs, r_hT, wr2, T)
                bGc = galloc()
                proj(bGc, slice(0, T), 128, wgc, cc * 128, hs, r_hT, wr2, T)
                p.op("act", lambda: A.activation(out=sg[0][:, :T], in_=ps[bGa][:, :T], func=AF.Tanh, scale=0.5), reads=[r_ps[bGa]], writes=[r_sg[0]])
                p.op("act", lambda: A.activation(out=sg[0][:, :T], in_=sg[0][:, :T], func=AF.Identity, scale=0.5, bias=0.5), reads=[r_sg[0]], writes=[r_sg[0]])
                p.op("act", lambda: A.activation(out=sg[1][:, :T], in_=ps[bGc][:, :T], func=AF.Tanh, scale=0.5), reads=[r_ps[bGc]], writes=[r_sg[1]])
                p.op("act", lambda: A.activation(out=sg[1][:, :T], in_=sg[1][:, :T], func=AF.Identity, scale=0.5, bias=0.5), reads=[r_sg[1]], writes=[r_sg[1]])
                p.op("dve", lambda: V.tensor_tensor(out=st[0][:, :T], in0=ps[bA][:, :T], in1=sg[0][:, :T], op=ALU.mult),
                     reads=[r_ps[bA], r_sg[0]], writes=[r_st[0]])
                p.op("dve", lambda: V.tensor_tensor(out=st[1][:, :T], in0=ps[bC][:, :T], in1=sg[1][:, :T], op=ALU.mult),
                     reads=[r_ps[bC], r_sg[1]], writes=[r_st[1]])
                p.op("dve", lambda: V.tensor_tensor(out=big[:, 16 + n, :T], in0=st[0][:, :T], in1=st[1][:, :T], op=ALU.add),
                     reads=[r_st[0], r_st[1]], writes=[r_big[16 + n]])
            yield None
        for n4 in range(2):
            (wv,), wr = wload((l, "wo", n4), [wcols(w_out[l], n4 * 512, 512)])
            for cc in range(4):
                n = 4 * n4 + cc
                b = galloc()
                proj(b, slice(0, T), 128, wv, cc * 128, [big[:, 16 + k, :T] for k in range(8)], r_big[16:24], wr, T)
                p.op("dve", lambda: V.tensor_tensor(out=xs_f[n], in0=xs_f[n], in1=ps[b][:, :T], op=ALU.add),
                     reads=[r_xT[n], r_ps[b]], writes=[r_xT[n]])
            yield None

        rms_rstd(l, xs_f, r_xT, D, T)
        for k in range(KC):
            p.op("dve", lambda: V.scalar_tensor_tensor(out=hs[k], in0=xs_f[k], scalar=vcol(l, V_NF + k), in1=rstd[:, :T],
                                                       op0=ALU.mult, op1=ALU.mult),
                 reads=[r_xT[k], r_rstd, r_vecs], writes=[r_hT[k]])
        ffw = {}

        def ffn_s1(j):
            i, cc, bs = j // 2, j % 2, j % 3
            if cc == 0:
                ffw["v"] = wload((l, "up", i), [wcols(w_up[l], i * 256, 256), wcols(w_up[l], cfg.DFF + i * 256, 256)])
            (wg_, wv_), wr = ffw["v"]
            for half, wsrc in enumerate((wg_, wv_)):
                jj = j + half * NFF
                ub, r_ub = upb[2 * bs + half], r_upb[2 * bs + half]
                acc, r_acc = ybuf[:, 2 * bs + half, :T], r_ybuf[2 * bs + half]
                b = galloc()
                proj(b, slice(0, T), 128, wsrc, cc * 128, hs, r_hT, wr, T)
                p.op("act", lambda: [A.copy(out=ub[:, 2:2 + T], in_=ps[b][:, :T]),
                                     A.copy(out=ub[:, 0:2], in_=fh[:, jj, :])],
                     reads=[r_ps[b], r_fh[jj], r_fhall], writes=[r_ub])
                p.op("act", lambda: A.activation(out=acc, in_=ps[b][:, :T], func=AF.Identity,
                                                 scale=vcol(l, V_FW + 2 * 44 + jj), bias=vcol(l, V_FB + jj)),
                     reads=[r_ps[b], r_vecs], writes=[r_acc])
                p.op("act", lambda: A.copy(out=fh[:, jj, :], in_=ub[:, T:T + 2]), reads=[r_ub], writes=[r_fh[jj]])

        def ffn_s2a(j):
            bs = j % 3
            for kk in (1, 0):
                for half in range(2):
                    jj = j + half * NFF
                    ub, r_ub = upb[2 * bs + half], r_upb[2 * bs + half]
                    acc, r_acc = ybuf[:, 2 * bs + half, :T], r_ybuf[2 * bs + half]
                    p.op("dve", lambda: V.scalar_tensor_tensor(out=acc, in0=ub[:, kk:kk + T], scalar=vcol(l, V_FW + kk * 44 + jj),
                                                               in1=acc, op0=ALU.mult, op1=ALU.add),
                         reads=[r_ub, r_acc], writes=[r_acc])

        def ffn_s2b(j):
            bs = j % 3
            p.op("act", lambda: A.activation(out=sg[bs][:, :T], in_=ybuf[:, 2 * bs, :T], func=AF.Silu),
                 reads=[r_ybuf[2 * bs]], writes=[r_sg[bs]])

        def ffn_s2c(j):
            bs = j % 3
            p.op("dve", lambda: V.tensor_tensor(out=big[:, j, :T], in0=sg[bs][:, :T], in1=ybuf[:, 2 * bs + 1, :T], op=ALU.mult),
                 reads=[r_sg[bs], r_ybuf[2 * bs + 1]], writes=[r_big[j]])

        for it in range(NFF + 2):
            if it < NFF:
                ffn_s1(it)
            if 0 <= it - 1 < NFF:
                ffn_s2a(it - 1)
                ffn_s2b(it - 1)
            if 0 <= it - 2 < NFF:
                ffn_s2c(it - 2)
            if it % 2 == 1 or it >= NFF:
                yield None
        if last_tile:
            o_fst = (o_fst_p if prompt else o_fst_s)[l]
            p.dma("pool", r_fhall, o_fst.rearrange("(k p) s -> p k s", p=128), fh[:], reads=r_fh + [r_fhall])
        prod = [big[:, k, :T] for k in range(NFF)]
        for n in range(KC):
            (wv,), wr = wload((l, "dn", n), [wcols(w_down[l], n * 128, 128)])
            b = galloc()
            proj(b, slice(0, T), 128, wv, 0, prod, r_big[0:NFF], wr, T)
            p.op("dve", lambda: V.tensor_tensor(out=xs_f[n], in0=xs_f[n], in1=ps[b][:, :T], op=ALU.add),
                 reads=[r_xT[n], r_ps[b]], writes=[r_xT[n]])
            yield None

        if l < L - 1:
            if prompt:
                p.dma("pool", r_xTall, xmid.rearrange("(k p) s -> p k s", p=128)[:, :, tok0:tok0 + T], xT[:, :, :T],
                      reads=r_xT + [r_xTall], writes=[r_xmid[t]])
        else:
            rms_rstd(l, xs_f, r_xT, D, T)
            for k in range(KC):
                p.op("dve", lambda: V.scalar_tensor_tensor(out=ybuf[:, k, :T], in0=xs_f[k], scalar=vcol(l, V_FIN + k), in1=rstd[:, :T],
                                                           op0=ALU.mult, op1=ALU.mult),
                     reads=[r_xT[k], r_rstd, r_vecs], writes=[r_ybuf[k]])
            yT = yT_p if prompt else yT_s
            p.dma("pool", r_ybuf[0], yT.rearrange("(k p) s -> p k s", p=128)[:, :, tok0:tok0 + T], ybuf[:, :, :T], reads=r_ybuf)

    r_xmid = [Res("xmid%d" % i) for i in range(NT)]
    xs_keep = sb("xs_keep", [128, KC, TS], F32)
    r_xskeep = Res("xskeep")
    for l in range(L):
        p.op("dve", lambda: V.memset(ubuf[:, :, 0:HW_], 0.0), writes=r_ubuf)
        p.op("dve", lambda: V.memset(fh[:], 0.0), writes=r_fh + [r_fhall])
        def advance(g, until):
            for v in g:
                if v == until:
                    return True
            return False

        def step(g, until):
            try:
                v = next(g)
            except StopIteration:
                return "end"
            return "hit" if v == until else "run"

        cur = tile_pass(l, 0, "p")
        advance(cur, "SEG2")
        for t in range(NT):
            advance(cur, "SEG3")
            nxt = tile_pass(l, t + 1, "p") if t + 1 < NT else None
            a_live, b_live = True, nxt is not None
            while a_live or b_live:
                if a_live:
                    for _ in range(2):
                        if step(cur, None) == "end":
                            a_live = False
                            break
                if b_live:
                    if step(nxt, "SEG2") in ("hit", "end"):
                        b_live = False
            cur = nxt
        allc = r_ckvT + r_ckvK + r_kpeT + [r_cacheall]
        p.dmas("pool", r_cacheall,
               [(ckvT[:, :, 0:PAST], cacheT[l].rearrange("(k p) s -> p k s", p=128)),
                (ckvK[:, 0:PAST // 128, 0:256], cache_tok[l].rearrange("(kt p) c -> p kt c", p=128)),
                (kpeT[0:64, 0:PAST], kcacheT[l])], writes=allc)
        p.dma("pool", r_ubuf[0], ubuf[:, :, 0:HW_], sconvT[l].rearrange("(k p) s -> p k s", p=128), writes=r_ubuf)
        p.dma("pool", r_fhall, fh[:], sffnT[l].rearrange("(k p) s -> p k s", p=128), writes=r_fh + [r_fhall])
        for _ in tile_pass(l, 0, "s"):
            pass
        if l < L - 1:
            p.op("act", lambda: A.copy(out=xs_keep[:, :, :], in_=xT2[0][:, :, :TS]), reads=r_xT2[0] + [r_xTall2[0]], writes=[r_xskeep])
    p.finish("sp")
    return nc, p


def _rope_tables(cfg, pos):
    half = cfg.ROPE // 2
    inv = np.power(np.float32(cfg.THETA), -np.arange(half, dtype=np.float32) / np.float32(half)).astype(np.float32)
    ang = (pos.astype(np.float32)[:, None] * inv[None, :]).astype(np.float32)
    c = np.cos(ang).astype(np.float32).T
    s = np.sin(ang).astype(np.float32).T
    cos2 = np.ascontiguousarray(np.concatenate([c, c], axis=0))
    sin2 = np.ascontiguousarray(np.concatenate([s, -s], axis=0))
    return cos2, sin2


def _cols(v):
    return np.ascontiguousarray(v.reshape(-1, 128).T)


def _pack_vecs(cfg, l, I):
    out = np.zeros((128, NV), np.float32)
    out[:, V_NM:V_NM + 8] = _cols(I["norm_mix"][l])
    out[:, V_QN:V_QN + 3] = _cols(I["q_norm"][l])
    out[:, V_KN:V_KN + 2] = _cols(I["kv_norm"][l])
    out[:, V_CB:V_CB + 8] = _cols(I["conv_dw_b"][l])
    out[:, V_LG:V_LG + 8] = _cols(I["conv_ln_g"][l])
    out[:, V_LB:V_LB + 8] = _cols(I["conv_ln_b"][l])
    out[:, V_NF:V_NF + 8] = _cols(I["norm_ffn"][l])
    out[:, V_FB:V_FB + 44] = _cols(I["ffn_dw_b"][l])
    out[:, V_CW:V_CW + 248] = I["conv_dw"][l].reshape(31, 8, 128).transpose(2, 0, 1).reshape(128, 248)
    out[:, V_FW:V_FW + 132] = I["ffn_dw"][l].reshape(3, 44, 128).transpose(2, 0, 1).reshape(128, 132)
    out[:, V_FIN:V_FIN + 8] = _cols(I["norm_final"])
    return out


_CACHE = {}


def kernel(_cfg=None, **inputs):
    cfg = _cfg or Cfg
    I = {k: np.asarray(v) for k, v in inputs.items()}
    L, NC = cfg.DEPTH, cfg.NCORES
    f32 = np.float32
    key = (cfg.SEQ, cfg.PAST, cfg.TP)
    if key not in _CACHE:
        _CACHE[key] = build_program(cfg)
    nc, _ = _CACHE[key]

    cos_p, sin_p = _rope_tables(cfg, np.arange(cfg.SEQ))
    cos_s, sin_s = _rope_tables(cfg, cfg.PAST + np.arange(cfg.DEC_SEQ))
    vecs = np.ascontiguousarray(np.stack([_pack_vecs(cfg, l, I) for l in range(L)], axis=1))
    w_kvb = I["w_kvb"]
    w_ukT = np.ascontiguousarray(
        w_kvb.reshape(L, cfg.KVL, cfg.NH, 256)[:, :, :, :128].transpose(0, 3, 2, 1).reshape(L, 128, cfg.NH * 256))
    shared = {
        "vecs": vecs, "w_in": I["w_in"], "w_qb": I["w_qb"], "w_kvb": w_kvb, "w_ukT": w_ukT,
        "w_o_attn": I["w_o_attn"], "w_conv_out": I["w_conv_out"], "w_out": I["w_out"],
        "w_up": I["w_up"], "w_down": I["w_down"],
        "cos_p": cos_p, "sin_p": sin_p, "cos_s": cos_s, "sin_s": sin_s,
        "ident": np.eye(128, dtype=f32),
    }
    in_maps = []
    for b in range(NC):
        m = dict(shared)
        m["xT_p"] = np.ascontiguousarray(I["x_prompt"][b].T)
        m["xT_s"] = np.ascontiguousarray(I["x_sample"][b].T)
        m["cacheT"] = np.ascontiguousarray(I["cache_kv_latent"][:, b].transpose(0, 2, 1))
        m["cache_tok"] = np.ascontiguousarray(I["cache_kv_latent"][:, b])
        m["kcacheT"] = np.ascontiguousarray(I["cache_k_rope"][:, b].transpose(0, 2, 1))
        m["sconvT"] = np.ascontiguousarray(I["state_conv"][:, b].transpose(0, 2, 1))
        m["sffnT"] = np.ascontiguousarray(I["state_ffn_conv"][:, b].transpose(0, 2, 1))
        in_maps.append(m)
    res = run_bass_kernel_spmd(nc, in_maps, core_ids=list(range(NC)))
    R = res.results

    def gather(name, tr):
        return np.ascontiguousarray(np.stack([np.asarray(R[b][name]).transpose(*tr) for b in range(NC)], axis=0))

    y_p = gather("yT_p", (1, 0))
    y_s = gather("yT_s", (1, 0))

    def gl(name):
        return np.ascontiguousarray(np.stack([np.asarray(R[b][name]).transpose(0, 2, 1) for b in range(NC)], axis=1))

    return (y_p.astype(f32), y_s.astype(f32),
            gl("o_ckv_p"), gl("o_kpe_p"), gl("o_cst_p"), gl("o_fst_p"),
            gl("o_ckv_s"), gl("o_kpe_s"), gl("o_cst_s"), gl("o_fst_s"))
```
